# Optimizing a Trainium2 kernel written in Bass

```python
import jax
import jax.numpy as jnp
from jax import lax
import numpy as np

D_MODEL = 2048
BATCH = 8
SEQ = 2048
DEPTH = 4

GRID_W = 64
CTX_LEN = 256
EPS = 1e-6
ROPE_THETA = 10000.0
ROPE_DIM = 64

MLA_HEADS = 4
MLA_NOPE = 128
MLA_ROPE = ROPE_DIM
MLA_V = 128
MLA_Q_RANK = 512
MLA_KV_RANK = 512
MLA_QK = MLA_NOPE + MLA_ROPE
Q_BLOCK = 128

GQA_HEADS = 8
GQA_KV_HEADS = 2
GQA_GROUP = GQA_HEADS // GQA_KV_HEADS
GQA_HD = ROPE_DIM
WINDOW = 128
BLOCK = 128

CONV_CH = 512
CONV_K = 31

POOL_WINDOWS = (2, 4, 8, 16)
POOL_GROUPS = 4
POOL_GC = 128
POOL_CH = POOL_GROUPS * POOL_GC

N_BRANCH = 4
BRANCH_W = 512

D_FF = 5504
FFN_CONV_K = 3

N_MOD = 6

KV_SPLITS = (MLA_KV_RANK, MLA_KV_RANK + MLA_ROPE, MLA_KV_RANK + MLA_ROPE + GQA_KV_HEADS * GQA_HD)
KV_COLS = MLA_KV_RANK + MLA_ROPE + 2 * GQA_KV_HEADS * GQA_HD
Q_COLS = MLA_Q_RANK + GQA_HEADS * GQA_HD
CONV_COLS = 2 * CONV_CH
GATE_COLS = N_BRANCH * D_MODEL
IN_SPLITS = (KV_COLS, KV_COLS + Q_COLS, KV_COLS + Q_COLS + CONV_COLS, KV_COLS + Q_COLS + CONV_COLS + POOL_CH)
IN_COLS = KV_COLS + Q_COLS + CONV_COLS + POOL_CH + GATE_COLS

kernel_name = 'hybrid_dit_prefix_ctx_gated_parallel_mixers'


def _rmsnorm(x, g):
    xf = x.astype(jnp.float32)
    y = xf * lax.rsqrt(jnp.mean(xf * xf, axis=-1, keepdims=True) + EPS)
    return (y * g.astype(jnp.float32)).astype(x.dtype)


def _layernorm(x, g, b):
    xf = x.astype(jnp.float32)
    mu = jnp.mean(xf, axis=-1, keepdims=True)
    var = jnp.mean(jnp.square(xf - mu), axis=-1, keepdims=True)
    y = (xf - mu) * lax.rsqrt(var + EPS) * g.astype(jnp.float32) + b.astype(jnp.float32)
    return y.astype(x.dtype)


def _axial_angles(seq):
    rows = seq // GRID_W
    row = jnp.repeat(jnp.arange(rows, dtype=jnp.float32), GRID_W)
    col = jnp.tile(jnp.arange(GRID_W, dtype=jnp.float32), rows)
    half = ROPE_DIM // 2
    inv_freq = ROPE_THETA ** (-jnp.arange(0, half, 2, dtype=jnp.float32) / half)
    return row[:, None] * inv_freq[None, :], col[:, None] * inv_freq[None, :]


def _rope_half(x, ang):
    n = x.shape[-1] // 2
    x1, x2 = x[..., :n], x[..., n:]
    cos = jnp.cos(ang)[:, None, :]
    sin = jnp.sin(ang)[:, None, :]
    return jnp.concatenate([x1 * cos - x2 * sin, x2 * cos + x1 * sin], axis=-1)


def _rope2d(x, ang_row, ang_col):
    xf = x.astype(jnp.float32)
    h = x.shape[-1] // 2
    y = jnp.concatenate([_rope_half(xf[..., :h], ang_row), _rope_half(xf[..., h:], ang_col)], axis=-1)
    return y.astype(x.dtype)


def _dwconv(x, w):
    k, ch = w.shape
    return lax.conv_general_dilated(
        x, w[:, None, :].astype(x.dtype), window_strides=(1,), padding=[(k // 2, k // 2)],
        dimension_numbers=('NWC', 'WIO', 'NWC'), feature_group_count=ch)


def _pool_mixer(u, w_pool, scale):
    b, l, ch = u.shape
    uf = u.astype(jnp.float32)
    cs = jnp.concatenate([jnp.zeros_like(uf[:, :1]), jnp.cumsum(uf, axis=1)], axis=1)
    t = jnp.arange(l)
    parts = []
    for gi, w in enumerate(POOL_WINDOWS):
        sl = slice(gi * POOL_GC, (gi + 1) * POOL_GC)
        lo = jnp.clip(t - w // 2, 0, l)
        hi = jnp.clip(t - w // 2 + w, 0, l)
        csg = cs[:, :, sl]
        mean = (csg[:, hi] - csg[:, lo]) / (hi - lo).astype(jnp.float32)[:, None]
        parts.append(mean - uf[:, :, sl])
    p = jnp.stack(parts, axis=2).astype(u.dtype)
    y = jnp.einsum('blgc,gcd->blgd', p, w_pool).reshape(b, l, ch)
    return y * scale


def _mla_q(zq, q_norm, w_uq, angs):
    b, l, _ = zq.shape
    q = (_rmsnorm(zq, q_norm) @ w_uq).reshape(b, l, MLA_HEADS, MLA_QK)
    q_nope, q_rope = q[..., :MLA_NOPE], q[..., MLA_NOPE:]
    if angs is not None:
        q_rope = _rope2d(q_rope, *angs)
    return jnp.concatenate([q_nope, q_rope], axis=-1)


def _mixer_kv(z_kv, angs, kv_norm, w_ukv):
    b, l, _ = z_kv.shape
    ckv, k_rope, gk, gv = jnp.split(z_kv, KV_SPLITS, axis=-1)
    kv = (_rmsnorm(ckv, kv_norm) @ w_ukv).reshape(b, l, MLA_HEADS, MLA_NOPE + MLA_V)
    k_nope, mv = kv[..., :MLA_NOPE], kv[..., MLA_NOPE:]
    k_rope = k_rope[:, :, None, :]
    gk = gk.reshape(b, l, GQA_KV_HEADS, GQA_HD)
    gv = gv.reshape(b, l, GQA_KV_HEADS, GQA_HD)
    if angs is not None:
        k_rope = _rope2d(k_rope, *angs)
        gk = _rope2d(gk, *angs)
    mk = jnp.concatenate([k_nope, jnp.broadcast_to(k_rope, (b, l, MLA_HEADS, MLA_ROPE))], axis=-1)
    return mk, mv, gk, gv


def _mla_ctx_attn(q, k, v):
    b, l = q.shape[:2]
    s = jnp.einsum('blhd,bjhd->bhlj', q, k, preferred_element_type=jnp.float32) * MLA_QK ** -0.5
    p = jax.nn.softmax(s, axis=-1).astype(v.dtype)
    return jnp.einsum('bhlj,bjhd->blhd', p, v).reshape(b, l, -1)


def _mla_latent_attn(q, k, v):
    b, s, h, dq = q.shape
    nb = s // Q_BLOCK
    qb = q.reshape(b, nb, Q_BLOCK, h, dq).transpose(1, 0, 2, 3, 4)

    def one_block(qblk):
        sc = jnp.einsum('bqhd,bkhd->bhqk', qblk, k, preferred_element_type=jnp.float32) * MLA_QK ** -0.5
        p = jax.nn.softmax(sc, axis=-1).astype(v.dtype)
        return jnp.einsum('bhqk,bkhd->bqhd', p, v)

    o = lax.map(one_block, qb)
    return o.transpose(1, 0, 2, 3, 4).reshape(b, s, -1)


def _gqa_window_attn(q, k, v, kc, vc, sink):
    b, s = q.shape[:2]
    nb = s // BLOCK
    pad = ((0, 0), (BLOCK, BLOCK), (0, 0), (0, 0))
    kp = jnp.pad(k, pad).reshape(b, nb + 2, BLOCK, GQA_KV_HEADS, GQA_HD)
    vp = jnp.pad(v, pad).reshape(b, nb + 2, BLOCK, GQA_KV_HEADS, GQA_HD)
    kb = jnp.concatenate([kp[:, :-2], kp[:, 1:-1], kp[:, 2:]], axis=2)
    vb = jnp.concatenate([vp[:, :-2], vp[:, 1:-1], vp[:, 2:]], axis=2)
    qb = q.reshape(b, nb, BLOCK, GQA_KV_HEADS, GQA_GROUP, GQA_HD)
    scale = GQA_HD ** -0.5
    s_win = jnp.einsum('bnqkgd,bnjkd->bnkgqj', qb, kb, preferred_element_type=jnp.float32) * scale
    s_ctx = jnp.einsum('bnqkgd,bjkd->bnkgqj', qb, kc, preferred_element_type=jnp.float32) * scale
    blk = jnp.arange(nb)[:, None, None] * BLOCK
    qpos = blk + jnp.arange(BLOCK)[None, :, None]
    kpos = blk - BLOCK + jnp.arange(3 * BLOCK)[None, None, :]
    valid = (kpos >= 0) & (kpos < s) & (jnp.abs(qpos - kpos) <= WINDOW)
    s_win = jnp.where(valid[None, :, None, None], s_win, -1e30)
    s_sink = jnp.broadcast_to(sink.astype(jnp.float32)[None, None, :, :, None, None], s_win.shape[:-1] + (1,))
    p = jax.nn.softmax(jnp.concatenate([s_sink, s_ctx, s_win], axis=-1), axis=-1)
    lc = kc.shape[1]
    p_ctx = p[..., 1:1 + lc].astype(v.dtype)
    p_win = p[..., 1 + lc:].astype(v.dtype)
    o = (jnp.einsum('bnkgqj,bjkd->bnqkgd', p_ctx, vc)
         + jnp.einsum('bnkgqj,bnjkd->bnqkgd', p_win, vb))
    return o.reshape(b, s, -1)


def _gqa_ctx_attn(q, k, v, sink):
    b, l = q.shape[:2]
    sc = jnp.einsum('blkgd,bjkd->bkglj', q, k, preferred_element_type=jnp.float32) * GQA_HD ** -0.5
    s_sink = jnp.broadcast_to(sink.astype(jnp.float32)[None, :, :, None, None], sc.shape[:-1] + (1,))
    p = jax.nn.softmax(jnp.concatenate([s_sink, sc], axis=-1), axis=-1)[..., 1:].astype(v.dtype)
    return jnp.einsum('bkglj,bjkd->blkgd', p, v).reshape(b, l, -1)


def _mix_stream(z, own_kv, ctx_kv, angs, mla_q_norm, mla_w_uq, gqa_sink, conv_w, conv_b,
                conv_ln_g, conv_ln_b, pool_w, pool_scale, w_branch, w_out):
    b, l, _ = z.shape
    _, zq, zconv, zpool, zgate = jnp.split(z, IN_SPLITS, axis=-1)
    zq_mla, zq_gqa = jnp.split(zq, [MLA_Q_RANK], axis=-1)
    mk, mv, gk, gv = own_kv
    a, g = jnp.split(zconv, 2, axis=-1)
    u = _dwconv(a * jax.nn.sigmoid(g), conv_w) + conv_b
    o_conv = jax.nn.silu(_layernorm(u, conv_ln_g, conv_ln_b))
    qm = _mla_q(zq_mla, mla_q_norm, mla_w_uq, angs)
    qg = zq_gqa.reshape(b, l, GQA_HEADS, GQA_HD)
    if angs is not None:
        qg = _rope2d(qg, *angs)
    qg = qg.reshape(b, l, GQA_KV_HEADS, GQA_GROUP, GQA_HD)
    sink = gqa_sink.reshape(GQA_KV_HEADS, GQA_GROUP)
    if ctx_kv is None:
        o_mla = _mla_ctx_attn(qm, mk, mv)
        o_gqa = _gqa_ctx_attn(qg, gk, gv, sink)
    else:
        cmk, cmv, cgk, cgv = ctx_kv
        o_mla = _mla_latent_attn(qm, jnp.concatenate([cmk, mk], axis=1), jnp.concatenate([cmv, mv], axis=1))
        o_gqa = _gqa_window_attn(qg, gk, gv, cgk, cgv, sink)
    o_pool = _pool_mixer(zpool, pool_w, pool_scale)
    gates = jax.nn.sigmoid(zgate.reshape(b, l, N_BRANCH, D_MODEL))
    branches = (o_conv, o_mla, o_gqa, o_pool)
    y = gates[:, :, 0] * (branches[0] @ w_branch[0])
    for n in range(1, N_BRANCH):
        y = y + gates[:, :, n] * (branches[n] @ w_branch[n])
    return y @ w_out


def _conv_ffn(h, w_up, conv_w, w_down):
    u = _dwconv(h @ w_up, conv_w)
    a, g = jnp.split(u, 2, axis=-1)
    return (jax.nn.silu(g) * a) @ w_down


def _layer(x, ctx, c, c_ctx, angs, norm1_g, norm2_g, w_ada, b_ada, w_in, mla_q_norm, mla_w_uq,
           mla_kv_norm, mla_w_ukv, gqa_sink, conv_w, conv_b, conv_ln_g, conv_ln_b, pool_w, pool_scale,
           w_branch, w_out, ffn_w_up, ffn_conv_w, ffn_w_down, last):
    b = x.shape[0]
    mod_x = (jax.nn.silu(c) @ w_ada + b_ada).reshape(b, N_MOD, D_MODEL)[:, :, None, :]
    mod_c = (jax.nn.silu(c_ctx) @ w_ada + b_ada).reshape(N_MOD, D_MODEL)
    sh1, sc1, gt1, sh2, sc2, gt2 = (mod_x[:, k] for k in range(N_MOD))
    csh1, csc1, cgt1, csh2, csc2, cgt2 = (mod_c[k] for k in range(N_MOD))

    hx = _rmsnorm(x, norm1_g) * (1 + sc1) + sh1
    hc = _rmsnorm(ctx, norm1_g) * (1 + csc1) + csh1
    zx = hx @ w_in
    zc = hc @ (w_in[:, :KV_COLS] if last else w_in)
    kv_c = _mixer_kv(zc[..., :KV_COLS], None, mla_kv_norm, mla_w_ukv)
    kv_x = _mixer_kv(zx[..., :KV_COLS], angs, mla_kv_norm, mla_w_ukv)
    mix_args = (mla_q_norm, mla_w_uq, gqa_sink, conv_w, conv_b, conv_ln_g, conv_ln_b,
                pool_w, pool_scale, w_branch, w_out)

    x = x + gt1 * _mix_stream(zx, kv_x, kv_c, angs, *mix_args)
    x = x + gt2 * _conv_ffn(_rmsnorm(x, norm2_g) * (1 + sc2) + sh2, ffn_w_up, ffn_conv_w, ffn_w_down)
    if not last:
        ctx = ctx + cgt1 * _mix_stream(zc, kv_c, None, None, *mix_args)
        ctx = ctx + cgt2 * _conv_ffn(_rmsnorm(ctx, norm2_g) * (1 + csc2) + csh2, ffn_w_up, ffn_conv_w, ffn_w_down)
    return x, ctx


def setup_inputs(seed: int = 0) -> dict:
    key = jax.random.key(seed)
    ks = iter(jax.random.split(key, 32))

    def nrm(shape, scale):
        return jax.random.normal(next(ks), shape, jnp.float32) * scale

    def gain(shape):
        return 1.0 + nrm(shape, 0.05)

    L = DEPTH
    return {
        'x': nrm((BATCH, SEQ, D_MODEL), 1.0),
        'c': nrm((BATCH, D_MODEL), 1.0),
        'ctx': nrm((BATCH, CTX_LEN, D_MODEL), 1.0),
        'c_ctx': nrm((D_MODEL,), 1.0),
        'norm1_g': gain((L, D_MODEL)),
        'norm2_g': gain((L, D_MODEL)),
        'w_ada': nrm((L, D_MODEL, N_MOD * D_MODEL), 0.5 * D_MODEL ** -0.5),
        'b_ada': nrm((L, N_MOD * D_MODEL), 0.02),
        'w_in': nrm((L, D_MODEL, IN_COLS), D_MODEL ** -0.5),
        'mla_q_norm': gain((L, MLA_Q_RANK)),
        'mla_w_uq': nrm((L, MLA_Q_RANK, MLA_HEADS * MLA_QK), MLA_Q_RANK ** -0.5),
        'mla_kv_norm': gain((L, MLA_KV_RANK)),
        'mla_w_ukv': nrm((L, MLA_KV_RANK, MLA_HEADS * (MLA_NOPE + MLA_V)), MLA_KV_RANK ** -0.5),
        'gqa_sink': nrm((L, GQA_HEADS), 0.5),
        'conv_w': nrm((L, CONV_K, CONV_CH), CONV_K ** -0.5),
        'conv_b': nrm((L, CONV_CH), 0.02),
        'conv_ln_g': gain((L, CONV_CH)),
        'conv_ln_b': nrm((L, CONV_CH), 0.02),
        'pool_w': nrm((L, POOL_GROUPS, POOL_GC, POOL_GC), POOL_GC ** -0.5),
        'pool_scale': gain((L, POOL_CH)),
        'w_branch': nrm((L, N_BRANCH, BRANCH_W, D_MODEL), BRANCH_W ** -0.5),
        'w_out': nrm((L, D_MODEL, D_MODEL), D_MODEL ** -0.5),
        'ffn_w_up': nrm((L, D_MODEL, 2 * D_FF), D_MODEL ** -0.5),
        'ffn_conv_w': nrm((L, FFN_CONV_K, 2 * D_FF), FFN_CONV_K ** -0.5),
        'ffn_w_down': nrm((L, D_FF, D_MODEL), D_FF ** -0.5),
        'final_norm_g': gain((D_MODEL,)),
    }


def reference(x, c, ctx, c_ctx, norm1_g, norm2_g, w_ada, b_ada, w_in, mla_q_norm, mla_w_uq,
              mla_kv_norm, mla_w_ukv, gqa_sink, conv_w, conv_b, conv_ln_g, conv_ln_b, pool_w,
              pool_scale, w_branch, w_out, ffn_w_up, ffn_conv_w, ffn_w_down, final_norm_g):
    angs = _axial_angles(x.shape[1])
    for i in range(DEPTH):
        x, ctx = _layer(
            x, ctx, c, c_ctx, angs, norm1_g[i], norm2_g[i], w_ada[i], b_ada[i], w_in[i],
            mla_q_norm[i], mla_w_uq[i], mla_kv_norm[i], mla_w_ukv[i], gqa_sink[i],
            conv_w[i], conv_b[i], conv_ln_g[i], conv_ln_b[i], pool_w[i], pool_scale[i],
            w_branch[i], w_out[i], ffn_w_up[i], ffn_conv_w[i], ffn_w_down[i], i == DEPTH - 1)
    return _rmsnorm(x, final_norm_g)
```

```python
import contextlib
import numpy as np
import concourse.bass as bass
import concourse.mybir as mybir
from concourse.bass_utils import run_bass_kernel_spmd

F32 = mybir.dt.float32
BF16 = mybir.dt.bfloat16
AF = mybir.ActivationFunctionType
ALU = mybir.AluOpType

L = 4
D = 2048
KC = 16
T = 2304
NLAT = 2048
NCTX = 256
TCH = [(0, 512), (512, 512), (1024, 512), (1536, 512), (2048, 256)]
IN_COLS = 11584
DFF = 5504
FCH = 43
EPS = 1e-6
N_DMA_SEMS = 24

DEBUG = {"on": False, "stop": None}


class Sched:
    COMPUTE = ("pe", "act", "dve", "pool")
    ALLENG = ("pe", "act", "dve", "pool", "sp")
    MAP = {"pe": "tensor", "act": "scalar", "dve": "vector", "pool": "gpsimd", "sp": "sync"}

    def __init__(self, nc, stack):
        self.nc = nc
        self.ops = []
        self.last_writer = {}
        self.readers = {}
        self.n_dma = 0
        self.emitted = 0
        self.cnt = {e: 0 for e in self.COMPUTE}
        self.waited = {e: {} for e in self.ALLENG}
        self.sems = {}
        for e in self.COMPUTE:
            self.sems[("c", e)] = stack.enter_context(nc.semaphore("s_" + e))
        for i in range(N_DMA_SEMS):
            self.sems[("dma", i)] = stack.enter_context(nc.semaphore("s_dma%d" % i))
        self.last_on = {}
        self.dma_hist = []

    def add(self, eng, fn, reads=(), writes=(), dma=False):
        idx = len(self.ops)
        deps = set()
        for r in reads:
            w = self.last_writer.get(r)
            if w is not None:
                deps.add(w)
        for w_ in writes:
            w = self.last_writer.get(w_)
            if w is not None:
                deps.add(w)
            for rd in self.readers.get(w_, ()):
                deps.add(rd)
        deps.discard(idx)
        op = dict(eng=eng, fn=fn, deps=deps, dma=dma, signal=dma, idx=idx)
        if dma:
            op["dma_i"] = self.n_dma
            self.n_dma += 1
            self.dma_hist.append(idx)
        self.ops.append(op)
        for r in reads:
            self.readers.setdefault(r, []).append(idx)
        for w_ in writes:
            self.last_writer[w_] = idx
            self.readers[w_] = []
        if fn is not None:
            self.last_on[eng] = idx
        return idx

    def pe(self, fn, reads=(), writes=()):
        return self.add("pe", fn, reads, writes)

    def act(self, fn, reads=(), writes=()):
        return self.add("act", fn, reads, writes)

    def dve(self, fn, reads=(), writes=()):
        return self.add("dve", fn, reads, writes)

    def dma(self, q, fn, reads=(), writes=()):
        return self.add(q, fn, reads, writes, dma=True)

    def barrier(self):
        deps = set(self.last_on.values()) | set(self.dma_hist[-N_DMA_SEMS:])
        for e in self.ALLENG:
            idx = self.add(e, None)
            self.ops[idx]["deps"] = set(d for d in deps if d >= self.emitted)
        self.last_writer = {}
        self.readers = {}

    def emit(self, final=False):
        nc = self.nc
        ops = self.ops
        start = self.emitted
        new = ops[start:]
        for op in new:
            op["deps"] = set(d for d in op["deps"] if d >= start)
            for d in op["deps"]:
                ops[d]["signal"] = True
        for op in new:
            if op["dma"]:
                i = op["dma_i"]
                op["sem"] = ("dma", i % N_DMA_SEMS)
                op["val"] = 16 * (i // N_DMA_SEMS + 1)
            elif op["signal"]:
                self.cnt[op["eng"]] += 1
                op["sem"] = ("c", op["eng"])
                op["val"] = self.cnt[op["eng"]]
        sems = self.sems

        def emit_engine(ename, e):
            waited = self.waited[ename]
            my = [op for op in new if op["eng"] == ename]
            for op in my:
                need = {}
                for d in op["deps"]:
                    dop = ops[d]
                    if dop["eng"] == "pe" and ename == "pe" and not dop["dma"]:
                        continue
                    k = dop["sem"]
                    need[k] = max(need.get(k, 0), dop["val"])
                if op["dma"]:
                    i = op["dma_i"]
                    if i >= N_DMA_SEMS:
                        k = ("dma", i % N_DMA_SEMS)
                        need[k] = max(need.get(k, 0), 16 * (i // N_DMA_SEMS))
                for k, v in sorted(need.items()):
                    if waited.get(k, 0) >= v:
                        continue
                    e.wait_ge(sems[k], v)
                    waited[k] = v
                if op["fn"] is None:
                    continue
                ins = op["fn"](e)
                if op["signal"]:
                    ins.then_inc(sems[op["sem"]], 16 if op["dma"] else 1)
            if final:
                fin = {}
                for op in ops:
                    if op["dma"]:
                        fin[op["sem"]] = max(fin.get(op["sem"], 0), op["val"])
                for k, v in sorted(fin.items()):
                    if waited.get(k, 0) >= v:
                        continue
                    e.wait_ge(sems[k], v)
                    waited[k] = v

        with nc.Block() as block:
            for ename in self.ALLENG:
                def mk(ename):
                    def f(e):
                        emit_engine(ename, e)
                    return f
                getattr(block, self.MAP[ename])(mk(ename))
        self.emitted = len(ops)


VEC_BLOCKS = [("n1g", L * 16), ("n2g", L * 16), ("fng", 16), ("qn", L * 4), ("kvn", L * 4),
              ("cb", L * 4), ("lng", L * 4), ("lnb", L * 4), ("psc", L * 4),
              ("cw", L * 124), ("fcw", L * 258), ("bada", L * 96), ("cc", 32)]
VEC_BASE = {}
_r = 0
for _n, _c in VEC_BLOCKS:
    VEC_BASE[_n] = _r
    _r += _c
VEC_ROWS = ((_r + 127) // 128) * 128
VEC_TILES = VEC_ROWS // 128

GLW = 2364
PLW = 2336
FFW = 2308


def _make_consts():
    ident = np.eye(128, dtype=np.float32)
    ones = np.ones((128, 128), np.float32)
    perm = np.zeros((128, 128), np.float32)
    for o in range(128):
        jj = o % 32
        p = o + 16 if jj < 16 else o - 16
        perm[p, o] = 1.0
    mask = np.zeros((128, 384), np.float32)
    kk = np.arange(128)[:, None]
    qq = np.arange(128)[None, :]
    mask[:, 0:128] = np.where(kk <= qq, 0.0, -30000.0)
    mask[:, 256:384] = np.where(qq <= kk, 0.0, -30000.0)
    half = 32
    inv_freq = (np.float32(10000.0) ** (-np.arange(0, half, 2, dtype=np.float32) / np.float32(half))).astype(np.float32)
    t = np.arange(NLAT)
    row = (t // 64).astype(np.float32)
    col = (t % 64).astype(np.float32)
    cosT = np.zeros((128, NLAT), np.float32)
    sinT = np.zeros((128, NLAT), np.float32)
    for p in range(128):
        j = p % 64
        pos = row if j < 32 else col
        jj = j % 32
        fi = jj % 16
        ang = (pos * inv_freq[fi]).astype(np.float32)
        cosT[p] = np.cos(ang)
        s = np.sin(ang)
        sinT[p] = -s if jj < 16 else s
    rc = np.zeros((4, PLW), np.float32)
    for gi, w in enumerate((2, 4, 8, 16)):
        for (l, off) in ((NLAT, 8), (NCTX, 8 + NLAT + 16)):
            tt = np.arange(l)
            lo = np.clip(tt - w // 2, 0, l)
            hi = np.clip(tt - w // 2 + w, 0, l)
            rc[gi, off:off + l] = 1.0 / (hi - lo).astype(np.float32)
    return dict(c_ident=ident, c_ones=ones, c_perm=perm, c_mask=mask, c_cos=cosT, c_sin=sinT, c_rc=rc)


def _make_vecs(inp, b):
    rows = np.zeros((VEC_ROWS, 128), np.float32)

    def put(name, arr):
        a = np.ascontiguousarray(arr, dtype=np.float32).reshape(-1, 128)
        rows[VEC_BASE[name]:VEC_BASE[name] + a.shape[0]] = a

    put("n1g", inp["norm1_g"])
    put("n2g", inp["norm2_g"])
    put("fng", inp["final_norm_g"])
    put("qn", inp["mla_q_norm"])
    put("kvn", inp["mla_kv_norm"])
    put("cb", inp["conv_b"])
    put("lng", inp["conv_ln_g"])
    put("lnb", inp["conv_ln_b"])
    put("psc", inp["pool_scale"])
    put("cw", inp["conv_w"])
    put("fcw", inp["ffn_conv_w"])
    put("bada", inp["b_ada"])
    put("cc", np.stack([inp["c"][b], inp["c_ctx"]], 0))
    return rows


def build_program():
    nc = bass.Bass("TRN2", target_bir_lowering=False)
    dbg = DEBUG["on"]
    stop = DEBUG["stop"]

    def din(name, shape):
        return nc.dram_tensor(name, list(shape), F32, kind="ExternalInput").ap()

    x_d = din("x", [NLAT, D])
    ctx_d = din("ctx", [NCTX, D])
    vecs_d = din("vecs", [VEC_ROWS, 128])
    sink_d = din("sink", [L * 8])
    w_ada_d = din("w_ada", [L, D, 6 * D])
    w_in_d = din("w_in", [L, D, IN_COLS])
    w_uq_d = din("mla_w_uq", [L, 512, 768])
    w_ukv_d = din("mla_w_ukv", [L, 512, 1024])
    pool_w_d = din("pool_w", [L, 4, 128, 128])
    w_br_d = din("w_branch", [L, 4, 512, D])
    w_out_d = din("w_out", [L, D, D])
    w_up_d = din("ffn_w_up", [L, D, 2 * DFF])
    w_dn_d = din("ffn_w_down", [L, DFF, D])
    c_ident_d = din("c_ident", [128, 128])
    c_ones_d = din("c_ones", [128, 128])
    c_perm_d = din("c_perm", [128, 128])
    c_mask_d = din("c_mask", [128, 384])
    c_cos_d = din("c_cos", [128, NLAT])
    c_sin_d = din("c_sin", [128, NLAT])
    c_rc_d = din("c_rc", [4, PLW])
    out_d = nc.dram_tensor("out", [NLAT, D], F32, kind="ExternalOutput").ap()

    skind = "ExternalOutput" if dbg else "Internal"
    xT_d = nc.dram_tensor("s_xT", [KC, 128, T], F32, kind=skind).ap()
    brT_d = nc.dram_tensor("s_brT", [16, 128, T], BF16, kind=skind).ap()
    yT_d = nc.dram_tensor("s_yT", [KC, 128, T], BF16, kind=skind).ap()
    actT_d = nc.dram_tensor("s_actT", [FCH, 128, T], BF16, kind=skind).ap()
    rs_d = nc.dram_tensor("s_rs", [128, T], F32, kind="Internal").ap()
    hxT_d = nc.dram_tensor("s_hxT", [KC, 128, T], BF16, kind=skind).ap() if dbg else None

    _uc = [0]

    def un(name):
        _uc[0] += 1
        return "%s_%d" % (name, _uc[0])

    with contextlib.ExitStack() as gst:
        S = Sched(nc, gst)

        def gsb(name, shape, dt):
            return gst.enter_context(nc.sbuf_tensor(un(name), list(shape), dt))

        PS = gst.enter_context(nc.psum_tensor("PS", [128, 8, 512], F32))
        ident_f = gsb("ident_f", [128, 128], F32)
        ident_b = gsb("ident_b", [128, 128], BF16)
        ones_b = gsb("ones_b", [128, 128], BF16)
        perm_b = gsb("perm_b", [128, 128], BF16)
        mask_b = gsb("mask_b", [128, 384], BF16)
        vT = gsb("vT", [128, VEC_ROWS], F32)
        modT = gsb("modT", [128, L, 6, 16, 2], F32)
        geff = gsb("geff", [128, L, 2, 16, 2], F32)
        esT = gsb("esT", [128, L, 4], F32)
        A = gsb("A", [128, KC, T], BF16)
        silb = gsb("silb", [128, 2, 16], BF16)

        def vcol(name, idx):
            c = VEC_BASE[name] + idx
            return vT[:, c:c + 1]

        def psb(b):
            return PS[:, b, :]

        def pid(b):
            return ("ps", b)

        def mm_group(out_ap, pairs, reads, writes):
            def fn(e, pairs=pairs, out_ap=out_ap):
                n = len(pairs)
                ins = None
                for i, (a, b) in enumerate(pairs):
                    ins = e.matmul(out_ap, a, b, start=(i == 0), stop=(i == n - 1))
                return ins
            S.pe(fn, reads, writes)

        def end_phase(final=False):
            S.barrier()
            S.emit(final=final)

        done = {"stop": False}

        def check_stop(tag):
            if stop is not None and stop == tag:
                done["stop"] = True
            return done["stop"]

        def ada_finish(l, bank):
            bb = VEC_BASE["bada"] + l * 96
            def evmod(e):
                return e.tensor_tensor(
                    out=modT[:, l].rearrange("p m k s -> p (m k) s"),
                    in0=PS[:, bank, 0:192].rearrange("p (j s) -> p j s", s=2),
                    in1=vT[:, bb:bb + 96].unsqueeze(2).to_broadcast([128, 96, 2]), op=ALU.add)
            S.dve(evmod, reads=[pid(bank)], writes=[("modT", l)])
            for which, (gname, mi) in enumerate((("n1g", 1), ("n2g", 4))):
                gb = VEC_BASE[gname] + l * 16
                def evg(e, which=which, mi=mi, gb=gb):
                    return e.scalar_tensor_tensor(
                        out=geff[:, l, which], in0=modT[:, l, mi], scalar=1.0,
                        in1=vT[:, gb:gb + 16].unsqueeze(2).to_broadcast([128, 16, 2]),
                        op0=ALU.add, op1=ALU.mult)
                S.dve(evg, reads=[("modT", l)], writes=[("geff", l, which)])

        def prologue():
            with contextlib.ExitStack() as st:
                def sb(name, shape, dt):
                    return st.enter_context(nc.sbuf_tensor(un(name), list(shape), dt))
                S.dma("sp", lambda e: e.dma_start(out=ident_f[:], in_=c_ident_d[:, :]), writes=["ident_f"])
                S.dma("pool", lambda e: e.dma_start(out=ident_b[:], in_=c_ident_d[:, :]), writes=["ident_b"])
                S.dma("pool", lambda e: e.dma_start(out=ones_b[:], in_=c_ones_d[:, :]), writes=["ones_b"])
                S.dma("pool", lambda e: e.dma_start(out=perm_b[:], in_=c_perm_d[:, :]), writes=["perm_b"])
                S.dma("pool", lambda e: e.dma_start(out=mask_b[:], in_=c_mask_d[:, :]), writes=["mask_b"])
                vin = [sb("vin%d" % i, [128, 128], F32) for i in range(2)]
                for vt in range(VEC_TILES):
                    b = vt % 2
                    S.dma("sp", lambda e, vt=vt, b=b: e.dma_start(out=vin[b][:], in_=vecs_d[vt * 128:(vt + 1) * 128, :]),
                          writes=[("vin", b)])
                    S.pe(lambda e, b=b: e.transpose(PS[:, b, 0:128], vin[b][:], ident_f[:]),
                         reads=[("vin", b), "ident_f"], writes=[pid(b)])
                    S.act(lambda e, vt=vt, b=b: e.activation(out=vT[:, vt * 128:(vt + 1) * 128], in_=PS[:, b, 0:128], func=AF.Copy),
                          reads=[pid(b)], writes=["vT"])
                sk = sb("sk", [128, L * 8], F32)
                skb = bass.AP(tensor=sink_d.tensor, offset=0, ap=[[0, 128], [1, L * 8]])
                S.dma("sp", lambda e: e.dma_start(out=sk[:], in_=skb), writes=["sk"])
                S.act(lambda e: e.activation(out=sk[:], in_=sk[:], func=AF.Exp), reads=["sk"], writes=["sk"])
                skv = sk[:].rearrange("p (l h two) -> p l h two", l=L, two=2)
                S.dve(lambda e: e.tensor_copy(out=esT[0:64, :, :], in_=skv[0:64, :, :, 0]), reads=["sk"], writes=["esT0"])
                S.dve(lambda e: e.tensor_copy(out=esT[64:128, :, :], in_=skv[64:128, :, :, 1]), reads=["sk"], writes=["esT1"])
                xin = [sb("xin%d" % i, [128, D], F32) for i in range(2)]
                xst = [sb("xst%d" % i, [128, KC, 128], F32) for i in range(2)]
                xTv = xT_d.rearrange("k p t -> p k t")
                for tt in range(18):
                    b = tt % 2
                    src = x_d[tt * 128:(tt + 1) * 128, :] if tt < 16 else ctx_d[(tt - 16) * 128:(tt - 15) * 128, :]
                    S.dma("sp", lambda e, b=b, src=src: e.dma_start(out=xin[b][:], in_=src), writes=[("xin", b)])
                    for q in range(4):
                        bank = 2 + (tt * 4 + q) % 6
                        def tp(e, b=b, q=q, bank=bank):
                            ins = None
                            for j in range(4):
                                k = q * 4 + j
                                ins = e.transpose(PS[:, bank, j * 128:(j + 1) * 128], xin[b][:, k * 128:(k + 1) * 128], ident_f[:])
                            return ins
                        S.pe(tp, reads=[("xin", b), "ident_f"], writes=[pid(bank)])
                        eng = S.act if q % 2 == 0 else S.dve
                        if q % 2 == 0:
                            S.act(lambda e, b=b, q=q, bank=bank: e.activation(
                                out=xst[b][:, q * 4:(q + 1) * 4, :], in_=PS[:, bank, :].rearrange("p (j t) -> p j t", j=4), func=AF.Copy),
                                reads=[pid(bank)], writes=[("xst", b, q)])
                        else:
                            S.dve(lambda e, b=b, q=q, bank=bank: e.tensor_copy(
                                out=xst[b][:, q * 4:(q + 1) * 4, :], in_=PS[:, bank, :].rearrange("p (j t) -> p j t", j=4)),
                                reads=[pid(bank)], writes=[("xst", b, q)])
                    S.dma("sp", lambda e, b=b, tt=tt: e.dma_start(out=xTv[:, :, tt * 128:(tt + 1) * 128], in_=xst[b][:]),
                          reads=[("xst", b, q) for q in range(4)], writes=["xT"])
                sil = sb("sil", [128, 32], F32)
                ccb = VEC_BASE["cc"]
                S.act(lambda e: e.activation(out=sil[:], in_=vT[:, ccb:ccb + 32], func=AF.Silu), reads=["vT"], writes=["sil"])
                S.dve(lambda e: e.tensor_copy(out=silb[:].rearrange("p s k -> p (s k)"), in_=sil[:]), reads=["sil"], writes=["silb"])
                wa = [sb("wa%d" % i, [128, KC, 512], BF16) for i in range(3)]
                wv = w_ada_d[0].rearrange("(k p) n -> p k n", p=128)
                for nb in range(24):
                    wi = nb % 3
                    S.dma("pool", lambda e, wi=wi, wv=wv, nb=nb: e.dma_start(out=wa[wi][:], in_=wv[:, :, nb * 512:(nb + 1) * 512]),
                          writes=[("wa", wi)])
                    def adamm(e, wi=wi, nb=nb):
                        ins = None
                        for jj in range(4):
                            j = nb * 4 + jj
                            for k in range(KC):
                                ins = e.matmul(PS[:, 0, j * 2:j * 2 + 2], wa[wi][:, k, jj * 128:(jj + 1) * 128],
                                               silb[:, :, k], start=(k == 0), stop=(k == KC - 1))
                        return ins
                    S.pe(adamm, reads=[("wa", wi), "silb"], writes=[pid(0)])
                ada_finish(0, 0)
                end_phase()

        def norm_phase(l, which, have_stats=False):
            final = which == 2
            with contextlib.ExitStack() as st:
                def sb(name, shape, dt):
                    return st.enter_context(nc.sbuf_tensor(un(name), list(shape), dt))
                NT = NLAT if final else T
                tch = [c for c in TCH if c[0] < NT]
                xl = [sb("xl%d" % i, [128, NT], F32) for i in range(3)]
                sq = [sb("sq%d" % i, [128, NT], BF16) for i in range(2)]
                rstd = sb("rstd", [128, NT], F32)
                nb = len(tch)
                if have_stats:
                    S.dma("sp", lambda e: e.dma_start(out=rstd[:, 0:NT], in_=rs_d[:, 0:NT]), writes=["rstdf"])
                else:
                    for k in range(KC):
                        b = k % 3
                        S.dma("sp", lambda e, k=k, b=b: e.dma_start(out=xl[b][:, 0:NT], in_=xT_d[k][:, 0:NT]), writes=[("xl", b)])
                        S.act(lambda e, k=k, b=b: e.activation(out=sq[k % 2][:, 0:NT], in_=xl[b][:, 0:NT], func=AF.Square),
                              reads=[("xl", b)], writes=[("sq", k % 2)])
                        def ssmm(e, k=k):
                            ins = None
                            for ci, (t0, n) in enumerate(tch):
                                ins = e.matmul(PS[:, ci, 0:n], ones_b[:], sq[k % 2][:, t0:t0 + n], start=(k == 0), stop=(k == KC - 1))
                            return ins
                        S.pe(ssmm, reads=[("sq", k % 2)], writes=[pid(ci) for ci in range(nb)])
                    for ci, (t0, n) in enumerate(tch):
                        S.act(lambda e, ci=ci, t0=t0, n=n: e.activation(out=rstd[:, t0:t0 + n], in_=PS[:, ci, 0:n], func=AF.Sqrt,
                                                                         scale=1.0 / D, bias=eps_t[:, 0:1]),
                              reads=[pid(ci)], writes=[("rstd", ci)])
                    S.dve(lambda e: e.reciprocal(out=rstd[:, 0:NT], in_=rstd[:, 0:NT]),
                          reads=[("rstd", ci) for ci in range(nb)], writes=["rstdf"])
                if not final:
                    tmp = [sb("ntmp%d" % i, [128, T], F32) for i in range(2)]
                    mi = 0 if which == 0 else 3
                    for k in range(KC):
                        b = k % 3
                        S.dma("sp", lambda e, k=k, b=b: e.dma_start(out=xl[b][:, 0:NT], in_=xT_d[k][:, 0:NT]), writes=[("xl", b)])
                        tb = k % 2
                        S.dve(lambda e, b=b, tb=tb: e.tensor_tensor(out=tmp[tb][:, 0:NT], in0=xl[b][:, 0:NT], in1=rstd[:, 0:NT], op=ALU.mult),
                              reads=[("xl", b), "rstdf"], writes=[("tmp", tb)])
                        for s, (t0, n) in enumerate(((0, NLAT), (NLAT, NCTX))):
                            S.act(lambda e, k=k, tb=tb, s=s, t0=t0, n=n: e.activation(
                                out=A[:, k, t0:t0 + n], in_=tmp[tb][:, t0:t0 + n], func=AF.Identity,
                                scale=geff[:, l, which, k, s:s + 1], bias=modT[:, l, mi, k, s:s + 1]),
                                reads=[("tmp", tb)], writes=[("A", k, s)])
                        if dbg and which == 0:
                            S.dma("sp", lambda e, k=k: e.dma_start(out=hxT_d[k], in_=A[:, k, :]),
                                  reads=[("A", k, 0), ("A", k, 1)], writes=[("hxTd", k)])
                else:
                    xs = [sb("xs%d" % i, [128, 512], F32) for i in range(3)]
                    tmp = [sb("ntmp%d" % i, [128, 512], F32) for i in range(2)]
                    outb = [sb("outb%d" % i, [128, 4, D], F32) for i in range(2)]
                    it = 0
                    for q in range(4):
                        ob_ = q % 2
                        for k in range(KC):
                            b = it % 3
                            tb = it % 2
                            S.dma("sp", lambda e, k=k, b=b, q=q: e.dma_start(out=xs[b][:], in_=xT_d[k][:, q * 512:(q + 1) * 512]), writes=[("xs", b)])
                            gcol = vcol("fng", k)
                            S.dve(lambda e, b=b, tb=tb, q=q, gcol=gcol: e.scalar_tensor_tensor(
                                out=tmp[tb][:], in0=xs[b][:], scalar=gcol, in1=rstd[:, q * 512:(q + 1) * 512], op0=ALU.mult, op1=ALU.mult),
                                reads=[("xs", b), "rstdf"], writes=[("tmp", tb)])
                            bank = 4 + it % 4
                            def tp(e, tb=tb, bank=bank):
                                ins = None
                                for j in range(4):
                                    ins = e.transpose(PS[:, bank, j * 128:(j + 1) * 128], tmp[tb][:, j * 128:(j + 1) * 128], ident_f[:])
                                return ins
                            S.pe(tp, reads=[("tmp", tb)], writes=[pid(bank)])
                            dst = outb[ob_][:, :, k * 128:(k + 1) * 128]
                            src = PS[:, bank, :].rearrange("p (j c) -> p j c", j=4)
                            if it % 2 == 0:
                                S.act(lambda e, dst=dst, src=src: e.activation(out=dst, in_=src, func=AF.Copy),
                                      reads=[pid(bank)], writes=[("outb", ob_, k)])
                            else:
                                S.dve(lambda e, dst=dst, src=src: e.tensor_copy(out=dst, in_=src),
                                      reads=[pid(bank)], writes=[("outb", ob_, k)])
                            it += 1
                        for j in range(4):
                            tt = q * 4 + j
                            S.dma("sp", lambda e, tt=tt, j=j, ob_=ob_: e.dma_start(out=out_d[tt * 128:(tt + 1) * 128, :], in_=outb[ob_][:, j, :]),
                                  reads=[("outb", ob_, k) for k in range(KC)], writes=[("out", tt)])
                end_phase(final=final)

        def proj_block(wt, wid, c0, ncols, tc, bank, rid):
            t0, n = TCH[tc]
            pairs = [(wt[:, k, c0:c0 + ncols], A[:, k, t0:t0 + n]) for k in range(KC)]
            mm_group(PS[0:ncols, bank, 0:n], pairs, reads=[(wid, c0 // 128)] + rid, writes=[pid(bank)])

        def load_w(dst, srcview, col0, ncols, wid, only=None):
            for pc in range((ncols + 127) // 128):
                if only is not None and pc != only:
                    continue
                a = pc * 128
                b = min(ncols, a + 128)
                S.dma("pool", lambda e, a=a, b=b: e.dma_start(out=dst[:, :, a:b], in_=srcview[:, :, col0 + a:col0 + b]),
                      writes=[(wid, pc)])

        def conv_phase(l, fused_norm=False):
            with contextlib.ExitStack() as st:
                def sb(name, shape, dt):
                    return st.enter_context(nc.sbuf_tensor(un(name), list(shape), dt))
                glu = sb("glu", [128, 4, GLW], BF16)

                def gcol0(tc):
                    t0, n = TCH[tc]
                    return (15 + t0) if tc < 4 else (15 + NLAT + 30)

                with contextlib.ExitStack() as st1:
                    def sb1(name, shape, dt):
                        return st1.enter_context(nc.sbuf_tensor(un(name), list(shape), dt))
                    wv = w_in_d[l].rearrange("(k p) n -> p k n", p=128)
                    wA = sb1("wA", [128, KC, 512], BF16)
                    wG = sb1("wG", [128, KC, 512], BF16)
                    for pc in range(4):
                        load_w(wA, wv, 1856, 512, "wA", only=pc)
                        load_w(wG, wv, 2368, 512, "wG", only=pc)
                    S.dve(lambda e: e.memset(glu[:], 0.0), writes=["glu"])
                    sig = [sb1("sig%d" % i, [128, 512], F32) for i in range(2)]
                    if fused_norm:
                        xs = [sb1("xs%d" % i, [128, 4, 512], F32) for i in range(3)]
                        tn = [sb1("tn%d" % i, [128, 4, 512], F32) for i in range(2)]
                        rstd = sb1("rstd", [128, T], F32)
                        S.dma("sp", lambda e: e.dma_start(out=rstd[:], in_=rs_d[:, :]), writes=["rstdf"])
                        xTv4 = xT_d.rearrange("k p t -> p k t")
                        ni = 0
                        for tc in range(5):
                            t0, n = TCH[tc]
                            s = 1 if tc == 4 else 0
                            for kg in range(4):
                                b = ni % 3
                                tb = ni % 2
                                ni += 1
                                S.dma("sp", lambda e, b=b, kg=kg, t0=t0, n=n: e.dma_start(out=xs[b][:, :, 0:n], in_=xTv4[:, kg * 4:(kg + 1) * 4, t0:t0 + n]),
                                      writes=[("xs", b)])
                                S.dve(lambda e, b=b, tb=tb, t0=t0, n=n: e.tensor_tensor(
                                    out=tn[tb][:, :, 0:n], in0=xs[b][:, :, 0:n],
                                    in1=rstd[:, t0:t0 + n].unsqueeze(1).to_broadcast([128, 4, n]), op=ALU.mult),
                                    reads=[("xs", b), "rstdf"], writes=[("tn", tb)])
                                for j in range(4):
                                    k = kg * 4 + j
                                    S.act(lambda e, k=k, j=j, tb=tb, s=s, t0=t0, n=n: e.activation(
                                        out=A[:, k, t0:t0 + n], in_=tn[tb][:, j, 0:n], func=AF.Identity,
                                        scale=geff[:, l, 0, k, s:s + 1], bias=modT[:, l, 0, k, s:s + 1]),
                                        reads=[("tn", tb)], writes=[("A", k, tc)])
                    i = 0
                    for c in range(4):
                        for tc in range(5):
                            t0, n = TCH[tc]
                            ba, bg = (i % 4) * 2, (i % 4) * 2 + 1
                            arid = [("A", k, tc) for k in range(KC)] if fused_norm else []
                            proj_block(wA, "wA", c * 128, 128, tc, ba, arid)
                            proj_block(wG, "wG", c * 128, 128, tc, bg, arid)
                            sb_ = i % 2
                            S.act(lambda e, bg=bg, n=n, sb_=sb_: e.activation(out=sig[sb_][:, 0:n], in_=PS[:, bg, 0:n], func=AF.Sigmoid),
                                  reads=[pid(bg)], writes=[("sig", sb_)])
                            g0 = gcol0(tc)
                            S.dve(lambda e, ba=ba, n=n, sb_=sb_, c=c, g0=g0: e.tensor_tensor(
                                out=glu[:, c, g0:g0 + n], in0=sig[sb_][:, 0:n], in1=PS[:, ba, 0:n], op=ALU.mult),
                                reads=[pid(ba), ("sig", sb_), "glu"], writes=[("glu", c, tc)])
                            i += 1
                    end_phase()
                dg = sb("dg", [128, 4, 31, 128], BF16)
                for c in range(4):
                    for k in range(31):
                        col = vcol("cw", l * 124 + k * 4 + c)
                        S.dve(lambda e, c=c, k=k, col=col: e.tensor_scalar(out=dg[:, c, k, :], in0=ident_f[:], scalar1=col, scalar2=None, op0=ALU.mult),
                              writes=[("dg", c)])
                obr = sb("obr", [128, 4, T], BF16)
                uf = sb("uf", [128, 4, 512], F32)
                ub = sb("ub", [128, 4, 512], BF16)
                usq = sb("usq", [128, 4, 512], BF16)
                mean = sb("mean", [128, 512], F32)
                var = sb("var", [128, 512], F32)
                xc = [sb("xc%d" % i, [128, 512], F32) for i in range(2)]
                for tc in range(5):
                    t0, n = TCH[tc]
                    g0 = gcol0(tc) - 15
                    for c in range(4):
                        bank = 4 + c
                        pairs = [(dg[:, c, k, :], glu[:, c, g0 + k:g0 + k + n]) for k in range(31)]
                        mm_group(PS[:, bank, 0:n], pairs, reads=[("dg", c)], writes=[pid(bank)])
                        cb = vcol("cb", l * 4 + c)
                        S.act(lambda e, c=c, bank=bank, n=n, cb=cb: e.activation(out=uf[:, c, 0:n], in_=PS[:, bank, 0:n], func=AF.Identity, bias=cb),
                              reads=[pid(bank)], writes=[("uf", c)])
                        S.dve(lambda e, c=c, n=n: e.tensor_copy(out=ub[:, c, 0:n], in_=uf[:, c, 0:n]), reads=[("uf", c)], writes=[("ub", c)])
                        S.act(lambda e, c=c, n=n: e.activation(out=usq[:, c, 0:n], in_=uf[:, c, 0:n], func=AF.Square), reads=[("uf", c)], writes=[("usq", c)])
                    mm_group(PS[:, 0, 0:n], [(ones_b[:], ub[:, c, 0:n]) for c in range(4)], reads=[("ub", c) for c in range(4)], writes=[pid(0)])
                    mm_group(PS[:, 1, 0:n], [(ones_b[:], usq[:, c, 0:n]) for c in range(4)], reads=[("usq", c) for c in range(4)], writes=[pid(1)])
                    S.dve(lambda e, n=n: e.tensor_scalar(out=mean[:, 0:n], in0=PS[:, 0, 0:n], scalar1=1.0 / 512, scalar2=None, op0=ALU.mult),
                          reads=[pid(0)], writes=["mean"])
                    S.dve(lambda e, n=n: e.tensor_tensor(out=var[:, 0:n], in0=mean[:, 0:n], in1=mean[:, 0:n], op=ALU.mult), reads=["mean"], writes=["var"])
                    S.dve(lambda e, n=n: e.scalar_tensor_tensor(out=var[:, 0:n], in0=PS[:, 1, 0:n], scalar=1.0 / 512, in1=var[:, 0:n],
                                                                op0=ALU.mult, op1=ALU.subtract), reads=[pid(1), "var"], writes=["var"])
                    S.act(lambda e, n=n: e.activation(out=var[:, 0:n], in_=var[:, 0:n], func=AF.Sqrt, bias=eps_t[:, 0:1]), reads=["var"], writes=["var"])
                    S.dve(lambda e, n=n: e.reciprocal(out=var[:, 0:n], in_=var[:, 0:n]), reads=["var"], writes=["var"])
                    for c in range(4):
                        xb_ = c % 2
                        S.dve(lambda e, c=c, n=n, xb_=xb_: e.tensor_tensor(out=xc[xb_][:, 0:n], in0=uf[:, c, 0:n], in1=mean[:, 0:n], op=ALU.subtract),
                              reads=[("uf", c), "mean"], writes=[("xc", xb_)])
                        S.dve(lambda e, n=n, xb_=xb_: e.tensor_tensor(out=xc[xb_][:, 0:n], in0=xc[xb_][:, 0:n], in1=var[:, 0:n], op=ALU.mult),
                              reads=[("xc", xb_), "var"], writes=[("xc", xb_)])
                        lg, lb = vcol("lng", l * 4 + c), vcol("lnb", l * 4 + c)
                        S.act(lambda e, c=c, n=n, xb_=xb_, lg=lg, lb=lb, t0=t0: e.activation(
                            out=obr[:, c, t0:t0 + n], in_=xc[xb_][:, 0:n], func=AF.Silu, scale=lg, bias=lb),
                            reads=[("xc", xb_)], writes=[("obr", c)])
                for c in range(4):
                    S.dma("sp", lambda e, c=c: e.dma_start(out=brT_d[0 * 4 + c], in_=obr[:, c, :]), reads=[("obr", c)], writes=[("brT", c)])
                end_phase()

        def pool_phase(l):
            with contextlib.ExitStack() as st:
                def sb(name, shape, dt):
                    return st.enter_context(nc.sbuf_tensor(un(name), list(shape), dt))
                wv = w_in_d[l].rearrange("(k p) n -> p k n", p=128)
                wP = sb("wP", [128, KC, 512], BF16)
                load_w(wP, wv, 2880, 512, "wP")
                wpl = sb("wpl", [128, 4, 128], BF16)
                S.dma("pool", lambda e: e.dma_start(out=wpl[:], in_=pool_w_d[l].rearrange("g c d -> c g d")), writes=["wpl"])
                up = [sb("up%d" % i, [128, PLW], F32) for i in range(2)]
                s1 = sb("s1", [128, PLW], F32)
                s2 = sb("s2", [128, PLW], F32)
                rc = [sb("rc%d" % i, [128, PLW], F32) for i in range(2)]
                pp = [sb("pp%d" % i, [128, PLW], BF16) for i in range(2)]
                obr = sb("obr", [128, 4, T], BF16)
                for i in range(2):
                    S.dve(lambda e, i=i: e.memset(up[i][:], 0.0), writes=[("up", i)])

                def pcol0(tc):
                    t0, n = TCH[tc]
                    return (8 + t0) if tc < 4 else (8 + NLAT + 16)

                for g, w in enumerate((2, 4, 8, 16)):
                    ub_ = g % 2
                    rcb = bass.AP(tensor=c_rc_d.tensor, offset=g * PLW, ap=[[0, 128], [1, PLW]])
                    S.dma("sp", lambda e, ub_=ub_, rcb=rcb: e.dma_start(out=rc[ub_][:], in_=rcb), writes=[("rc", ub_)])
                    for tc in range(5):
                        t0, n = TCH[tc]
                        bank = (g * 5 + tc) % 4
                        proj_block(wP, "wP", g * 128, 128, tc, bank, [])
                        p0 = pcol0(tc)
                        S.act(lambda e, ub_=ub_, bank=bank, n=n, p0=p0: e.activation(out=up[ub_][:, p0:p0 + n], in_=PS[:, bank, 0:n], func=AF.Copy),
                              reads=[pid(bank)], writes=[("up", ub_)])
                    src = up[ub_]
                    width = 1
                    bufs = [s1, s2]
                    bi = 0
                    W = PLW
                    while width < w:
                        dst = bufs[bi]
                        nvalid = W - 2 * width + 1
                        S.dve(lambda e, src=src, dst=dst, width=width, nvalid=nvalid: e.tensor_tensor(
                            out=dst[:, 0:nvalid], in0=src[:, 0:nvalid], in1=src[:, width:width + nvalid], op=ALU.add),
                            reads=[("up", ub_), "s1", "s2"], writes=["s1" if bi == 0 else "s2"])
                        src = dst
                        width *= 2
                        bi ^= 1
                    h = w // 2
                    lo, hi = 8, PLW - 8
                    dstw = bufs[bi]
                    S.dve(lambda e, src=src, dstw=dstw, ub_=ub_, h=h, lo=lo, hi=hi: e.tensor_tensor(
                        out=dstw[:, lo:hi], in0=src[:, lo - h:hi - h], in1=rc[ub_][:, lo:hi], op=ALU.mult),
                        reads=["s1", "s2", ("rc", ub_)], writes=["s1" if bi == 0 else "s2"])
                    S.dve(lambda e, dstw=dstw, ub_=ub_, lo=lo, hi=hi: e.tensor_tensor(
                        out=pp[ub_][:, lo:hi], in0=dstw[:, lo:hi], in1=up[ub_][:, lo:hi], op=ALU.subtract),
                        reads=["s1", "s2", ("up", ub_)], writes=[("pp", ub_)])
                    for tc in range(5):
                        t0, n = TCH[tc]
                        bank = 4 + (g * 5 + tc) % 4
                        p0 = pcol0(tc)
                        mm_group(PS[:, bank, 0:n], [(wpl[:, g, :], pp[ub_][:, p0:p0 + n])], reads=["wpl", ("pp", ub_)], writes=[pid(bank)])
                        sc = vcol("psc", l * 4 + g)
                        S.act(lambda e, g=g, bank=bank, n=n, t0=t0, sc=sc: e.activation(out=obr[:, g, t0:t0 + n], in_=PS[:, bank, 0:n], func=AF.Copy, scale=sc),
                              reads=[pid(bank), "vT"], writes=[("obr", g)])
                for c in range(4):
                    S.dma("sp", lambda e, c=c: e.dma_start(out=brT_d[3 * 4 + c], in_=obr[:, c, :]), reads=[("obr", c)], writes=[("brT", c)])
                end_phase()

        def proj_rmsnorm(wt, wid, gname, l, dst, dstid, raw, sqt, rs):
            for tc in range(5):
                t0, n = TCH[tc]
                for c in range(4):
                    bank = c % 2
                    proj_block(wt, wid, c * 128, 128, tc, bank, [])
                    S.act(lambda e, c=c, bank=bank, n=n, t0=t0: e.activation(out=raw[:, c, 0:n], in_=PS[:, bank, 0:n], func=AF.Copy),
                          reads=[pid(bank)], writes=[("raw", c)])
                    S.act(lambda e, c=c, bank=bank, n=n: e.activation(out=sqt[c % 2][:, 0:n], in_=PS[:, bank, 0:n], func=AF.Square),
                          reads=[pid(bank)], writes=[("sqt", c % 2)])
                    S.pe(lambda e, c=c, n=n: e.matmul(PS[:, 2, 0:n], ones_b[:], sqt[c % 2][:, 0:n], start=(c == 0), stop=(c == 3)),
                         reads=[("sqt", c % 2), "ones_b"], writes=[pid(2)])
                S.act(lambda e, n=n: e.activation(out=rs[:, 0:n], in_=PS[:, 2, 0:n], func=AF.Sqrt, scale=1.0 / 512, bias=eps_t[:, 0:1]),
                      reads=[pid(2), "eps"], writes=["rs"])
                S.dve(lambda e, n=n: e.reciprocal(out=rs[:, 0:n], in_=rs[:, 0:n]), reads=["rs"], writes=["rs"])
                for c in range(4):
                    gc = vcol(gname, l * 4 + c)
                    S.dve(lambda e, c=c, n=n, t0=t0, gc=gc: e.scalar_tensor_tensor(
                        out=dst[:, c, t0:t0 + n], in0=raw[:, c, 0:n], scalar=gc, in1=rs[:, 0:n], op0=ALU.mult, op1=ALU.mult),
                        reads=[("raw", c), "rs", "vT"], writes=[(dstid, c, tc)])

        def rope_chunk(raw_ap, rawids, dst_ap, dstids, t0, n, cosb, sinb, rt, bank, slot):
            mm_group(PS[:, bank, 0:n], [(perm_b[:], raw_ap)], reads=["perm_b"] + rawids, writes=[pid(bank)])
            S.dve(lambda e: e.tensor_tensor(out=rt[slot][:, 0:n], in0=sinb[:, t0:t0 + n], in1=PS[:, bank, 0:n], op=ALU.mult),
                  reads=[pid(bank), "sin"], writes=[("rt", slot)])
            S.dve(lambda e: e.tensor_tensor(out=rt[2 + slot][:, 0:n], in0=raw_ap, in1=cosb[:, t0:t0 + n], op=ALU.mult),
                  reads=rawids + ["cos"], writes=[("rt", 2 + slot)])
            S.dve(lambda e: e.tensor_tensor(out=dst_ap, in0=rt[slot][:, 0:n], in1=rt[2 + slot][:, 0:n], op=ALU.add),
                  reads=[("rt", slot), ("rt", 2 + slot)], writes=dstids)

        def load_rope_tables(sb):
            cosb = sb("cosb", [128, NLAT], BF16)
            sinb = sb("sinb", [128, NLAT], BF16)
            S.dma("pool", lambda e: e.dma_start(out=cosb[:], in_=c_cos_d[:, :]), writes=["cos"])
            S.dma("pool", lambda e: e.dma_start(out=sinb[:], in_=c_sin_d[:, :]), writes=["sin"])
            return cosb, sinb

        def mla_phase(l):
            with contextlib.ExitStack() as st:
                def sb(name, shape, dt):
                    return st.enter_context(nc.sbuf_tensor(un(name), list(shape), dt))
                wv = w_in_d[l].rearrange("(k p) n -> p k n", p=128)
                knope = sb("knope", [128, 4, T], BF16)
                vtok = sb("vtok", [128, 18, 512], BF16)
                kr = sb("kr", [128, T], BF16)
                with contextlib.ExitStack() as st1:
                    def sb1(name, shape, dt):
                        return st1.enter_context(nc.sbuf_tensor(un(name), list(shape), dt))
                    wC = sb1("wC", [128, KC, 512], BF16)
                    wKR = sb1("wKR", [128, KC, 64], BF16)
                    wUKV = sb1("wUKV", [128, 4, 1024], BF16)
                    load_w(wC, wv, 0, 512, "wC")
                    S.dma("pool", lambda e: e.dma_start(out=wKR[:], in_=wv[:, :, 512:576]), writes=["wKR"])
                    S.dma("pool", lambda e: e.dma_start(out=wUKV[:], in_=w_ukv_d[l].rearrange("(kc p) n -> p kc n", p=128)), writes=["wUKV"])
                    cosb, sinb = load_rope_tables(sb1)
                    raw = sb1("raw", [128, 4, 512], BF16)
                    sqt = [sb1("sqt%d" % i, [128, 512], BF16) for i in range(2)]
                    rs = sb1("rs", [128, 512], F32)
                    cn = sb1("cn", [128, 4, T], BF16)
                    krr = [sb1("krr%d" % i, [128, 512], BF16) for i in range(2)]
                    rt = [sb1("rt%d" % i, [128, 512], F32) for i in range(4)]
                    proj_rmsnorm(wC, "wC", "kvn", l, cn, "cn", raw, sqt, rs)
                    i = 0
                    for h in range(4):
                        for tc in range(5):
                            t0, n = TCH[tc]
                            bank = 4 + i % 4
                            mm_group(PS[:, bank, 0:n], [(wUKV[:, kc, h * 256:h * 256 + 128], cn[:, kc, t0:t0 + n]) for kc in range(4)],
                                     reads=["wUKV"] + [("cn", kc, tc) for kc in range(4)], writes=[pid(bank)])
                            S.act(lambda e, h=h, bank=bank, t0=t0, n=n: e.activation(out=knope[:, h, t0:t0 + n], in_=PS[:, bank, 0:n], func=AF.Copy),
                                  reads=[pid(bank)], writes=[("knope", h, tc)])
                            i += 1
                    wmv = wUKV[:].rearrange("p kc (h two d) -> p kc h two d", two=2, d=128)
                    for tt in range(18):
                        bank = 4 + i % 4
                        tc = min(tt // 4, 4)
                        mm_group(PS[:, bank, :], [(cn[:, kc, tt * 128:(tt + 1) * 128], wmv[:, kc, :, 1, :]) for kc in range(4)],
                                 reads=["wUKV"] + [("cn", kc, tc) for kc in range(4)], writes=[pid(bank)])
                        S.dve(lambda e, tt=tt, bank=bank: e.tensor_copy(out=vtok[:, tt, :], in_=PS[:, bank, :].rearrange("p (h d) -> p h d", d=128)),
                              reads=[pid(bank)], writes=[("vtok", tt)])
                        i += 1
                    for tc in range(5):
                        t0, n = TCH[tc]
                        bank = 4 + i % 4
                        for hf in range(2):
                            mm_group(PS[hf * 64:(hf + 1) * 64, bank, 0:n], [(wKR[:, k, :], A[:, k, t0:t0 + n]) for k in range(KC)],
                                     reads=["wKR"], writes=[pid(bank)])
                        if tc < 4:
                            kb = tc % 2
                            S.act(lambda e, bank=bank, kb=kb, n=n: e.activation(out=krr[kb][:, 0:n], in_=PS[:, bank, 0:n], func=AF.Copy),
                                  reads=[pid(bank)], writes=[("krr", kb)])
                            rope_chunk(krr[kb][:, 0:n], [("krr", kb)], kr[:, t0:t0 + n], [("kr", tc)], t0, n, cosb, sinb, rt, 2 + kb, kb)
                        else:
                            S.act(lambda e, bank=bank, t0=t0, n=n: e.activation(out=kr[:, t0:t0 + n], in_=PS[:, bank, 0:n], func=AF.Copy),
                                  reads=[pid(bank)], writes=[("kr", tc)])
                        i += 1
                    end_phase()
                cq = sb("cq", [128, 4, T], BF16)
                with contextlib.ExitStack() as st1:
                    def sb1(name, shape, dt):
                        return st1.enter_context(nc.sbuf_tensor(un(name), list(shape), dt))
                    wQ = sb1("wQ", [128, KC, 512], BF16)
                    load_w(wQ, wv, 832, 512, "wQ")
                    raw = sb1("raw", [128, 4, 512], BF16)
                    sqt = [sb1("sqt%d" % i, [128, 512], BF16) for i in range(2)]
                    rs = sb1("rs", [128, 512], F32)
                    proj_rmsnorm(wQ, "wQ", "qn", l, cq, "cq", raw, sqt, rs)
                    end_phase()
                scale = 192.0 ** -0.5
                for hp in range(2):
                    with contextlib.ExitStack() as st1:
                        def sb1(name, shape, dt):
                            return st1.enter_context(nc.sbuf_tensor(un(name), list(shape), dt))
                        wUQ = sb1("wUQ", [128, 4, 768], BF16)
                        S.dma("pool", lambda e: e.dma_start(out=wUQ[:], in_=w_uq_d[l].rearrange("(kc p) n -> p kc n", p=128)), writes=["wUQ"])
                        cosb, sinb = load_rope_tables(sb1)
                        qn = sb1("qn", [128, 2, T], BF16)
                        qr = sb1("qr", [128, T], BF16)
                        qrr = [sb1("qrr%d" % i, [128, 512], BF16) for i in range(2)]
                        rt = [sb1("rt%d" % i, [128, 512], F32) for i in range(4)]
                        pT = [sb1("pT%d" % i, [128, 512], BF16) for i in range(3)]
                        rden = sb1("rden", [128, 512], F32)
                        obr = sb1("obr", [128, 2, T], BF16)
                        i = 0
                        for hf in range(2):
                            h = hp * 2 + hf
                            for tc in range(5):
                                t0, n = TCH[tc]
                                bank = 3 + i % 4
                                mm_group(PS[:, bank, 0:n], [(wUQ[:, kc, h * 192:h * 192 + 128], cq[:, kc, t0:t0 + n]) for kc in range(4)],
                                         reads=["wUQ"], writes=[pid(bank)])
                                S.act(lambda e, hf=hf, bank=bank, t0=t0, n=n: e.activation(out=qn[:, hf, t0:t0 + n], in_=PS[:, bank, 0:n], func=AF.Copy),
                                      reads=[pid(bank)], writes=[("qn", hf, tc)])
                                i += 1
                        for tc in range(5):
                            t0, n = TCH[tc]
                            bank = 3 + i % 4
                            for hf in range(2):
                                h = hp * 2 + hf
                                mm_group(PS[hf * 64:(hf + 1) * 64, bank, 0:n],
                                         [(wUQ[:, kc, h * 192 + 128:h * 192 + 192], cq[:, kc, t0:t0 + n]) for kc in range(4)],
                                         reads=["wUQ"], writes=[pid(bank)])
                            if tc < 4:
                                kb = tc % 2
                                S.act(lambda e, bank=bank, kb=kb, n=n: e.activation(out=qrr[kb][:, 0:n], in_=PS[:, bank, 0:n], func=AF.Copy),
                                      reads=[pid(bank)], writes=[("qrr", kb)])
                                rope_chunk(qrr[kb][:, 0:n], [("qrr", kb)], qr[:, t0:t0 + n], [("qr", tc)], t0, n, cosb, sinb, rt, 7, kb)
                            else:
                                S.act(lambda e, bank=bank, t0=t0, n=n: e.activation(out=qr[:, t0:t0 + n], in_=PS[:, bank, 0:n], func=AF.Copy),
                                      reads=[pid(bank)], writes=[("qr", tc)])
                            i += 1
                        qdeps = [("qn", hf, tc) for hf in range(2) for tc in range(5)] + [("qr", tc) for tc in range(5)]
                        for qc in range(5):
                            q0, n = TCH[qc]
                            tiles = list(range(18)) if qc < 4 else [16, 17]
                            items = [(hf, kt) for hf in range(2) for kt in tiles]

                            def emit_qk(idx, items=items, q0=q0, n=n):
                                hf, kt = items[idx]
                                h = hp * 2 + hf
                                sbank = idx % 3
                                pairs = [(knope[:, h, kt * 128:(kt + 1) * 128], qn[:, hf, q0:q0 + n]),
                                         (kr[hf * 64:(hf + 1) * 64, kt * 128:(kt + 1) * 128], qr[hf * 64:(hf + 1) * 64, q0:q0 + n])]
                                mm_group(PS[:, sbank, 0:n], pairs, reads=qdeps, writes=[pid(sbank)])
                                S.act(lambda e, sbank=sbank: e.activation(out=pT[sbank][:, 0:n], in_=PS[:, sbank, 0:n], func=AF.Exp, scale=scale),
                                      reads=[pid(sbank)], writes=[("pT", sbank)])

                            def emit_pv(idx, items=items, q0=q0, n=n, tiles=tiles):
                                hf, kt = items[idx]
                                h = hp * 2 + hf
                                sbank = idx % 3
                                ob = 3 + hf * 2
                                first = kt == tiles[0]
                                last = kt == tiles[-1]
                                S.pe(lambda e: e.matmul(PS[:, ob, 0:n], vtok[:, kt, h * 128:(h + 1) * 128], pT[sbank][:, 0:n], start=first, stop=last),
                                     reads=[("pT", sbank)], writes=[pid(ob)])
                                S.pe(lambda e: e.matmul(PS[:, ob + 1, 0:n], ones_b[:], pT[sbank][:, 0:n], start=first, stop=last),
                                     reads=[("pT", sbank)], writes=[pid(ob + 1)])
                                if last:
                                    S.dve(lambda e: e.reciprocal(out=rden[:, 0:n], in_=PS[:, ob + 1, 0:n]), reads=[pid(ob + 1)], writes=["rden"])
                                    S.dve(lambda e: e.tensor_tensor(out=obr[:, hf, q0:q0 + n], in0=PS[:, ob, 0:n], in1=rden[:, 0:n], op=ALU.mult),
                                          reads=[pid(ob), "rden"], writes=[("obr", hf)])

                            NI = len(items)
                            for idx in range(min(2, NI)):
                                emit_qk(idx)
                            for idx in range(NI):
                                if idx + 2 < NI:
                                    emit_qk(idx + 2)
                                emit_pv(idx)
                        for hf in range(2):
                            S.dma("sp", lambda e, hf=hf: e.dma_start(out=brT_d[1 * 4 + hp * 2 + hf], in_=obr[:, hf, :]), reads=[("obr", hf)], writes=[("brT", hf)])
                        end_phase()

        def gqa_phase(l):
            with contextlib.ExitStack() as st:
                def sb(name, shape, dt):
                    return st.enter_context(nc.sbuf_tensor(un(name), list(shape), dt))
                wv = w_in_d[l].rearrange("(k p) n -> p k n", p=128)
                wGQ = sb("wGQ", [128, KC, 512], BF16)
                wKV = sb("wKV", [128, KC, 256], BF16)
                load_w(wGQ, wv, 1344, 512, "wGQ")
                load_w(wKV, wv, 576, 256, "wKV")
                cosb, sinb = load_rope_tables(sb)
                gq = sb("gq", [128, 4, T], BF16)
                gk = sb("gk", [128, 2, T], BF16)
                gvt = sb("gvt", [128, 18, 128], BF16)
                rr = [sb("rr%d" % i, [128, 512], BF16) for i in range(2)]
                rt = [sb("rt%d" % i, [128, 512], F32) for i in range(4)]
                i = 0
                for c in range(4):
                    for tc in range(5):
                        t0, n = TCH[tc]
                        bank = 5 + i % 3
                        proj_block(wGQ, "wGQ", c * 128, 128, tc, bank, [])
                        if tc < 4:
                            kb = i % 2
                            S.act(lambda e, bank=bank, kb=kb, n=n: e.activation(out=rr[kb][:, 0:n], in_=PS[:, bank, 0:n], func=AF.Copy),
                                  reads=[pid(bank)], writes=[("rr", kb)])
                            rope_chunk(rr[kb][:, 0:n], [("rr", kb)], gq[:, c, t0:t0 + n], [("gq", c, tc)], t0, n, cosb, sinb, rt, 3 + kb, kb)
                        else:
                            S.act(lambda e, c=c, bank=bank, t0=t0, n=n: e.activation(out=gq[:, c, t0:t0 + n], in_=PS[:, bank, 0:n], func=AF.Copy),
                                  reads=[pid(bank)], writes=[("gq", c, tc)])
                        i += 1
                for kh in range(2):
                    for tc in range(5):
                        t0, n = TCH[tc]
                        bank = 5 + i % 3
                        for hf in range(2):
                            mm_group(PS[hf * 64:(hf + 1) * 64, bank, 0:n], [(wKV[:, k, kh * 64:(kh + 1) * 64], A[:, k, t0:t0 + n]) for k in range(KC)],
                                     reads=[("wKV", 0)], writes=[pid(bank)])
                        if tc < 4:
                            kb = i % 2
                            S.act(lambda e, bank=bank, kb=kb, n=n: e.activation(out=rr[kb][:, 0:n], in_=PS[:, bank, 0:n], func=AF.Copy),
                                  reads=[pid(bank)], writes=[("rr", kb)])
                            rope_chunk(rr[kb][:, 0:n], [("rr", kb)], gk[:, kh, t0:t0 + n], [("gk", kh, tc)], t0, n, cosb, sinb, rt, 3 + kb, kb)
                        else:
                            S.act(lambda e, kh=kh, bank=bank, t0=t0, n=n: e.activation(out=gk[:, kh, t0:t0 + n], in_=PS[:, bank, 0:n], func=AF.Copy),
                                  reads=[pid(bank)], writes=[("gk", kh, tc)])
                        i += 1
                for tg in range(5):
                    nt = 4 if tg < 4 else 2
                    bank = 5 + i % 3
                    def gvmm(e, tg=tg, nt=nt, bank=bank):
                        ins = None
                        for j in range(nt):
                            tt = tg * 4 + j
                            for k in range(KC):
                                ins = e.matmul(PS[:, bank, j * 128:(j + 1) * 128], A[:, k, tt * 128:(tt + 1) * 128], wKV[:, k, 128:256],
                                               start=(k == 0), stop=(k == KC - 1))
                        return ins
                    S.pe(gvmm, reads=[("wKV", 1)], writes=[pid(bank)])
                    S.dve(lambda e, tg=tg, nt=nt, bank=bank: e.tensor_copy(out=gvt[:, tg * 4:tg * 4 + nt, :],
                                                                              in_=PS[:, bank, 0:nt * 128].rearrange("p (j d) -> p j d", d=128)),
                          reads=[pid(bank)], writes=[("gvt", tg)])
                    i += 1
                pT = [sb("pT%d" % i, [128, 512], BF16) for i in range(3)]
                rden = sb("rden", [128, 512], F32)
                obr = sb("obr", [128, 4, T], BF16)
                qdeps = [("gq", c, tc) for c in range(4) for tc in range(5)]
                kdeps = [("gk", kh, tc) for kh in range(2) for tc in range(5)]
                vdeps = [("gvt", tg) for tg in range(5)]
                for qc in range(5):
                    q0, n = TCH[qc]
                    n0 = q0 // 128
                    for hp in range(4):
                        kh = hp // 2
                        ob = 3 + (hp % 2) * 2
                        items = []
                        for hf in range(2):
                            items.append((hf, NLAT, 16, 0, n, None))
                            items.append((hf, NLAT + 128, 17, 0, n, None))
                            if qc < 4:
                                for j in range(n0 - 1, n0 + 5):
                                    if j < 0 or j >= 16:
                                        continue
                                    b_lo = max(j - 1, n0)
                                    b_hi = min(j + 1, n0 + 3)
                                    c0 = (b_lo - n0) * 128
                                    ncol = (b_hi - b_lo + 1) * 128
                                    m0 = (b_lo - (j - 1)) * 128
                                    items.append((hf, j * 128, j, c0, ncol, m0))
                        NI = len(items)
                        firsts = {}
                        lasts = {}
                        for idx, itx in enumerate(items):
                            firsts.setdefault(itx[0], idx)
                            lasts[itx[0]] = idx

                        def emit_qk(idx, items=items, q0=q0, hp=hp, kh=kh):
                            hf, k0, tile, c0, ncol, m0 = items[idx]
                            sbank = idx % 3
                            P = slice(hf * 64, (hf + 1) * 64)
                            pairs = [(gk[P, kh, k0:k0 + 128], gq[P, hp, q0 + c0:q0 + c0 + ncol])]
                            if m0 is not None:
                                pairs.append((ident_b[:], mask_b[:, m0:m0 + ncol]))
                            mm_group(PS[:, sbank, 0:ncol], pairs, reads=kdeps + qdeps, writes=[pid(sbank)])
                            S.act(lambda e, sbank=sbank, ncol=ncol: e.activation(out=pT[sbank][:, 0:ncol], in_=PS[:, sbank, 0:ncol], func=AF.Exp, scale=0.125),
                                  reads=[pid(sbank)], writes=[("pT", sbank)])

                        def emit_pv(idx, items=items, firsts=firsts, lasts=lasts, ob=ob, kh=kh):
                            hf, k0, tile, c0, ncol, m0 = items[idx]
                            sbank = idx % 3
                            P = slice(hf * 64, (hf + 1) * 64)
                            first = firsts[hf] == idx
                            last = lasts[hf] == idx
                            S.pe(lambda e: e.matmul(PS[P, ob, c0:c0 + ncol], gvt[:, tile, kh * 64:(kh + 1) * 64], pT[sbank][:, 0:ncol],
                                                    start=first, stop=last, skip_group_check=True),
                                 reads=vdeps + [("pT", sbank)], writes=[pid(ob)])
                            S.pe(lambda e: e.matmul(PS[P, ob + 1, c0:c0 + ncol], ones_b[:, 0:64], pT[sbank][:, 0:ncol],
                                                    start=first, stop=last, skip_group_check=True),
                                 reads=[("pT", sbank)], writes=[pid(ob + 1)])

                        for idx in range(min(2, NI)):
                            emit_qk(idx)
                        for idx in range(NI):
                            if idx + 2 < NI:
                                emit_qk(idx + 2)
                            emit_pv(idx)
                        S.dve(lambda e, ob=ob, hp=hp, n=n: e.tensor_scalar(out=rden[:, 0:n], in0=PS[:, ob + 1, 0:n], scalar1=esT[:, l, hp:hp + 1], scalar2=None, op0=ALU.add),
                              reads=[pid(ob + 1)], writes=["rden"])
                        S.dve(lambda e, n=n: e.reciprocal(out=rden[:, 0:n], in_=rden[:, 0:n]), reads=["rden"], writes=["rden"])
                        S.dve(lambda e, ob=ob, hp=hp, n=n, q0=q0: e.tensor_tensor(out=obr[:, hp, q0:q0 + n], in0=PS[:, ob, 0:n], in1=rden[:, 0:n], op=ALU.mult),
                              reads=[pid(ob), "rden"], writes=[("obr", hp)])
                for c in range(4):
                    S.dma("sp", lambda e, c=c: e.dma_start(out=brT_d[2 * 4 + c], in_=obr[:, c, :]), reads=[("obr", c)], writes=[("brT", c)])
                end_phase()

        def merge_phase(l):
            with contextlib.ExitStack() as st:
                def sb(name, shape, dt):
                    return st.enter_context(nc.sbuf_tensor(un(name), list(shape), dt))
                B = sb("B", [128, 16, T], BF16)
                brv = brT_d.rearrange("j p t -> p j t")
                for j0 in range(0, 16, 4):
                    S.dma("sp", lambda e, j0=j0: e.dma_start(out=B[:, j0:j0 + 4, :], in_=brv[:, j0:j0 + 4, :]), writes=[("B", j) for j in range(j0, j0 + 4)])
                wv = w_in_d[l].rearrange("(k p) n -> p k n", p=128)
                wgv = wv[:, :, 3392:IN_COLS].rearrange("p k (n f c) -> p k n f c", n=4, f=16)
                wbv = w_br_d[l].rearrange("n (kc p) d -> p n kc d", p=128)
                NW = 4
                wg = [sb("wg%d" % i, [128, KC, 128], BF16) for i in range(NW)]
                wb = [sb("wb%d" % i, [128, 4, 128], BF16) for i in range(NW)]
                sg = [sb("sg%d" % i, [128, 512], F32) for i in range(2)]
                acc = sb("acc", [128, T], F32)
                tmp = [sb("mtmp%d" % i, [128, 512], F32) for i in range(2)]
                yst = [sb("yst%d" % i, [128, T], BF16) for i in range(2)]
                it = 0
                u = 0
                for f in range(16):
                    yi = f % 2
                    for n_ in range(4):
                        wi = u % NW
                        u += 1
                        S.dma("pool", lambda e, wi=wi, f=f, n_=n_: e.dma_start(out=wg[wi][:], in_=wgv[:, :, n_, f, :]), writes=[("wg", wi)])
                        S.dma("pool", lambda e, wi=wi, f=f, n_=n_: e.dma_start(out=wb[wi][:], in_=wbv[:, n_, :, f * 128:(f + 1) * 128]), writes=[("wb", wi)])
                        for tc in range(4 if l == L - 1 else 5):
                            t0, n = TCH[tc]
                            bg = (it % 4)
                            bb = 4 + (it % 4)
                            mm_group(PS[:, bg, 0:n], [(wg[wi][:, k, :], A[:, k, t0:t0 + n]) for k in range(KC)],
                                     reads=[("wg", wi)], writes=[pid(bg)])
                            mm_group(PS[:, bb, 0:n], [(wb[wi][:, kc, :], B[:, n_ * 4 + kc, t0:t0 + n]) for kc in range(4)],
                                     reads=[("wb", wi)] + [("B", n_ * 4 + kc) for kc in range(4)], writes=[pid(bb)])
                            sb_ = it % 2
                            S.act(lambda e, bg=bg, sb_=sb_, n=n: e.activation(out=sg[sb_][:, 0:n], in_=PS[:, bg, 0:n], func=AF.Sigmoid),
                                  reads=[pid(bg)], writes=[("sg", sb_)])
                            if n_ == 0:
                                S.dve(lambda e, bb=bb, sb_=sb_, n=n, t0=t0: e.tensor_tensor(out=acc[:, t0:t0 + n], in0=sg[sb_][:, 0:n], in1=PS[:, bb, 0:n], op=ALU.mult),
                                      reads=[pid(bb), ("sg", sb_)], writes=[("acc", tc)])
                            else:
                                S.dve(lambda e, bb=bb, sb_=sb_, n=n: e.tensor_tensor(out=tmp[sb_][:, 0:n], in0=sg[sb_][:, 0:n], in1=PS[:, bb, 0:n], op=ALU.mult),
                                      reads=[pid(bb), ("sg", sb_)], writes=[("mtmp", sb_)])
                                if n_ < 3:
                                    S.dve(lambda e, n=n, t0=t0, sb_=sb_: e.tensor_tensor(out=acc[:, t0:t0 + n], in0=acc[:, t0:t0 + n], in1=tmp[sb_][:, 0:n], op=ALU.add),
                                          reads=[("acc", tc), ("mtmp", sb_)], writes=[("acc", tc)])
                                else:
                                    S.dve(lambda e, n=n, t0=t0, yi=yi, sb_=sb_: e.tensor_tensor(out=yst[yi][:, t0:t0 + n], in0=acc[:, t0:t0 + n], in1=tmp[sb_][:, 0:n], op=ALU.add),
                                          reads=[("acc", tc), ("mtmp", sb_)], writes=[("yst", yi)])
                            it += 1
                    S.dma("sp", lambda e, f=f, yi=yi: e.dma_start(out=yT_d[f], in_=yst[yi][:]), reads=[("yst", yi)], writes=[("yT", f)])
                end_phase()

        def resid_phase(l, which):
            last = (l == L - 1)
            with contextlib.ExitStack() as st:
                def sb(name, shape, dt):
                    return st.enter_context(nc.sbuf_tensor(un(name), list(shape), dt))
                if which == 0:
                    nk = KC
                    halves = [(0, NLAT, [0, 1, 2, 3])] if last else [(0, T, [0, 1, 2, 3, 4])]
                    src_d = yT_d
                    wview = w_out_d[l].rearrange("(k p) n -> p k n", p=128)
                    mi = 2
                    HL = T

                    def R(j):
                        return A[:, j, :]
                    rgroups = [(A, 0, 4, 0), (A, 4, 8, 0), (A, 8, 12, 0), (A, 12, 16, 0)]
                else:
                    nk = FCH
                    halves = [(0, 1024, [0, 1]), (1024, 1024, [2, 3])] if last else [(0, 1024, [0, 1]), (1024, 1280, [2, 3, 4])]
                    src_d = actT_d
                    wview = w_dn_d[l].rearrange("(f p) n -> p f n", p=128)
                    mi = 5
                    HL = 1280
                    NA = 28
                    Av = A[:].rearrange("p k t -> p (k t)")[:, 0:NA * HL].rearrange("p (j t) -> p j t", t=HL)
                    R2 = sb("R2", [128, nk - NA, HL], BF16)

                    def R(j):
                        return Av[:, j, :] if j < NA else R2[:, j - NA, :]
                    rgroups = [(Av, 0, 7, 0), (Av, 7, 14, 0), (Av, 14, 21, 0), (Av, 21, 28, 0), (R2, 28, 36, NA), (R2, 36, 43, NA)]
                wo = [sb("wo%d" % i, [128, nk, 128], BF16) for i in range(3)]
                xo = [sb("xo%d" % i, [128, HL], F32) for i in range(2)]
                xn = [sb("xn%d" % i, [128, HL], F32) for i in range(2)]
                sqx = [sb("sqx%d" % i, [128, 512], BF16) for i in range(4)]
                rsa = [sb("rsa%d" % i, [128, 512], F32) for i in range(2)]
                rsb = [sb("rsb%d" % i, [128, 512], F32) for i in range(2)]
                it = 0
                mb = 0
                sc = 0
                pend = []

                def flush(keep):
                    while len(pend) > keep:
                        si, tc, n, o = pend.pop(0)
                        S.pe(lambda e, si=si, tc=tc, n=n, o=o: e.matmul(PS[:, 3 + tc, 0:n], ones_b[:], sqx[si][:, 0:n], start=(o == 0), stop=(o == 15)),
                             reads=[("sqx", si)], writes=[pid(3 + tc)])

                for hi_, (h0, hl, tcs) in enumerate(halves):
                    srcv = src_d.rearrange("j p t -> p j t")
                    for (dstv, j0, j1, joff) in rgroups:
                        S.dma("sp", lambda e, dstv=dstv, j0=j0, j1=j1, joff=joff, h0=h0, hl=hl: e.dma_start(
                            out=dstv[:, j0 - joff:j1 - joff, 0:hl], in_=srcv[:, j0:j1, h0:h0 + hl]),
                            writes=[("R", j) for j in range(j0, j1)])
                    for o in range(16):
                        wi = it % 3
                        xb = it % 2
                        S.dma("pool", lambda e, wi=wi, o=o: e.dma_start(out=wo[wi][:], in_=wview[:, :, o * 128:(o + 1) * 128]), writes=[("wo", wi)])
                        S.dma("sp", lambda e, xb=xb, o=o, h0=h0, hl=hl: e.dma_start(out=xo[xb][:, 0:hl], in_=xT_d[o][:, h0:h0 + hl]),
                              writes=[("xo", xb)])
                        for tci, tc in enumerate(tcs):
                            t0, n = TCH[tc]
                            bank = mb % 3
                            mb += 1
                            mm_group(PS[:, bank, 0:n], [(wo[wi][:, j, :], R(j)[:, t0 - h0:t0 - h0 + n]) for j in range(nk)],
                                     reads=[("wo", wi)] + [("R", j) for j in range(nk)], writes=[pid(bank)])
                            s = 1 if tc == 4 else 0
                            gt = modT[:, l, mi, o, s:s + 1]
                            S.dve(lambda e, bank=bank, n=n, xb=xb, t0=t0, h0=h0, gt=gt: e.scalar_tensor_tensor(
                                out=xn[xb][:, t0 - h0:t0 - h0 + n], in0=PS[:, bank, 0:n], scalar=gt, in1=xo[xb][:, t0 - h0:t0 - h0 + n],
                                op0=ALU.mult, op1=ALU.add), reads=[pid(bank), ("xo", xb)], writes=[("xn", xb, tc)])
                            si = sc % 4
                            sc += 1
                            S.act(lambda e, si=si, n=n, xb=xb, t0=t0, h0=h0: e.activation(out=sqx[si][:, 0:n], in_=xn[xb][:, t0 - h0:t0 - h0 + n], func=AF.Square),
                                  reads=[("xn", xb, tc)], writes=[("sqx", si)])
                            pend.append((si, tc, n, o))
                            flush(2)
                        S.dma("sp", lambda e, xb=xb, o=o, h0=h0, hl=hl: e.dma_start(out=xT_d[o][:, h0:h0 + hl], in_=xn[xb][:, 0:hl]),
                              reads=[("xn", xb, tc) for tc in tcs], writes=[("xTo", o)])
                        it += 1
                    flush(0)
                    for tci, tc in enumerate(tcs):
                        t0, n = TCH[tc]
                        rb = tci % 2
                        S.act(lambda e, tc=tc, n=n, rb=rb: e.activation(out=rsa[rb][:, 0:n], in_=PS[:, 3 + tc, 0:n], func=AF.Sqrt, scale=1.0 / D, bias=eps_t[:, 0:1]),
                              reads=[pid(3 + tc)], writes=[("rsa", rb)])
                        S.dve(lambda e, n=n, rb=rb: e.reciprocal(out=rsb[rb][:, 0:n], in_=rsa[rb][:, 0:n]), reads=[("rsa", rb)], writes=[("rsb", rb)])
                        S.dma("sp", lambda e, t0=t0, n=n, rb=rb: e.dma_start(out=rs_d[:, t0:t0 + n], in_=rsb[rb][:, 0:n]), reads=[("rsb", rb)], writes=[("rsd", tc)])
                end_phase()

        def ffn_up_phase(l):
            with contextlib.ExitStack() as st:
                def sb(name, shape, dt):
                    return st.enter_context(nc.sbuf_tensor(un(name), list(shape), dt))
                wv = w_up_d[l].rearrange("(k p) n -> p k n", p=128).rearrange("p k (two j c) -> p k two j c", two=2, j=FCH)
                wu = [sb("wu%d" % i, [128, KC, 2, 128], BF16) for i in range(3)]
                ua = [sb("ua%d" % i, [128, FFW], F32) for i in range(2)]
                ug = [sb("ug%d" % i, [128, FFW], F32) for i in range(2)]
                va0 = sb("va0", [128, FFW], F32)
                vg0 = sb("vg0", [128, FFW], F32)
                va = [va0, va0]
                vg = [vg0, vg0]
                ao = [sb("ao%d" % i, [128, T], BF16) for i in range(2)]
                do_ada = l < L - 1
                if do_ada:
                    wa2 = [sb("wa2_%d" % i, [128, KC, 128], BF16) for i in range(2)]
                    wav = w_ada_d[l + 1].rearrange("(k p) n -> p k n", p=128)
                ada_j = [0]

                def ada_block():
                    j = ada_j[0]
                    ada_j[0] += 1
                    wi2 = j % 2
                    S.dma("pool", lambda e, wi2=wi2, j=j: e.dma_start(out=wa2[wi2][:], in_=wav[:, :, j * 128:(j + 1) * 128]), writes=[("wa2", wi2)])
                    def adamm(e, wi2=wi2, j=j):
                        ins = None
                        for k in range(KC):
                            ins = e.matmul(PS[:, 7, j * 2:j * 2 + 2], wa2[wi2][:, k, :], silb[:, :, k], start=(k == 0), stop=(k == KC - 1))
                        return ins
                    S.pe(adamm, reads=[("wa2", wi2)], writes=[pid(7)])
                for i in range(2):
                    S.dve(lambda e, i=i: e.memset(ua[i][:], 0.0), writes=[("ua", i)])
                    S.dve(lambda e, i=i: e.memset(ug[i][:], 0.0), writes=[("ug", i)])

                def fcol0(tc):
                    t0, n = TCH[tc]
                    return (1 + t0) if tc < 4 else (1 + NLAT + 2)

                ntc = 4 if l == L - 1 else 5
                xs = [sb("xs%d" % i, [128, 2, 512], F32) for i in range(2)]
                tn = [sb("tn%d" % i, [128, 2, 512], F32) for i in range(2)]
                rstd = sb("rstd", [128, T], F32)
                S.dma("sp", lambda e: e.dma_start(out=rstd[:], in_=rs_d[:, :]), writes=["rstdf"])
                xTv4 = xT_d.rearrange("k p t -> p k t")
                ni = 0
                for tc in range(ntc):
                    t0, n = TCH[tc]
                    s = 1 if tc == 4 else 0
                    for kg in range(8):
                        b = ni % 2
                        ni += 1
                        S.dma("sp", lambda e, b=b, kg=kg, t0=t0, n=n: e.dma_start(out=xs[b][:, :, 0:n], in_=xTv4[:, kg * 2:(kg + 1) * 2, t0:t0 + n]),
                              writes=[("xs", b)])
                        S.dve(lambda e, b=b, t0=t0, n=n: e.tensor_tensor(
                            out=tn[b][:, :, 0:n], in0=xs[b][:, :, 0:n],
                            in1=rstd[:, t0:t0 + n].unsqueeze(1).to_broadcast([128, 2, n]), op=ALU.mult),
                            reads=[("xs", b), "rstdf"], writes=[("tn", b)])
                        for j in range(2):
                            k = kg * 2 + j
                            S.act(lambda e, k=k, j=j, b=b, s=s, t0=t0, n=n: e.activation(
                                out=A[:, k, t0:t0 + n], in_=tn[b][:, j, 0:n], func=AF.Identity,
                                scale=geff[:, l, 1, k, s:s + 1], bias=modT[:, l, 3, k, s:s + 1]),
                                reads=[("tn", b)], writes=[("A", k, tc)])

                it = 0
                for f in range(FCH):
                    wi = f % 3
                    ub_ = f % 2
                    for two in range(2):
                        S.dma("pool", lambda e, wi=wi, f=f, two=two: e.dma_start(out=wu[wi][:, :, two, :], in_=wv[:, :, two, f, :]), writes=[("wu", wi)])
                    for tc in range(ntc):
                        t0, n = TCH[tc]
                        f0 = fcol0(tc)
                        for two, dst, nm in ((0, ua, "ua"), (1, ug, "ug")):
                            bank = it % 7
                            mm_group(PS[:, bank, 0:n], [(wu[wi][:, k, two, :], A[:, k, t0:t0 + n]) for k in range(KC)],
                                     reads=[("wu", wi)] + [("A", k, tc) for k in range(KC)], writes=[pid(bank)])
                            if two == 0:
                                S.act(lambda e, dst=dst, bank=bank, n=n, f0=f0, ub_=ub_: e.activation(out=dst[ub_][:, f0:f0 + n], in_=PS[:, bank, 0:n], func=AF.Copy),
                                      reads=[pid(bank)], writes=[(nm, ub_)])
                            else:
                                S.dve(lambda e, dst=dst, bank=bank, n=n, f0=f0, ub_=ub_: e.tensor_copy(out=dst[ub_][:, f0:f0 + n], in_=PS[:, bank, 0:n]),
                                      reads=[pid(bank)], writes=[(nm, ub_)])
                            it += 1
                    if do_ada:
                        for _ in range(3 if f < 10 else 2):
                            if ada_j[0] < 96:
                                ada_block()
                    W = FFW - 2
                    for two, src, dst, nm, vn in ((0, ua, va, "ua", "va"), (1, ug, vg, "ug", "vg")):
                        ch = two * FCH + f
                        w0, w1, w2 = (vcol("fcw", l * 258 + tap * 86 + ch) for tap in range(3))
                        S.dve(lambda e, src=src, dst=dst, w0=w0, ub_=ub_: e.tensor_scalar(out=dst[ub_][:, 1:1 + W], in0=src[ub_][:, 0:W], scalar1=w0, scalar2=None, op0=ALU.mult),
                              reads=[(nm, ub_), "vT"], writes=[(vn, 0)])
                        S.dve(lambda e, src=src, dst=dst, w1=w1, ub_=ub_: e.scalar_tensor_tensor(out=dst[ub_][:, 1:1 + W], in0=src[ub_][:, 1:1 + W], scalar=w1, in1=dst[ub_][:, 1:1 + W],
                                                                                    op0=ALU.mult, op1=ALU.add), reads=[(nm, ub_), (vn, 0), "vT"], writes=[(vn, 0)])
                        S.dve(lambda e, src=src, dst=dst, w2=w2, ub_=ub_: e.scalar_tensor_tensor(out=dst[ub_][:, 1:1 + W], in0=src[ub_][:, 2:2 + W], scalar=w2, in1=dst[ub_][:, 1:1 + W],
                                                                                    op0=ALU.mult, op1=ALU.add), reads=[(nm, ub_), (vn, 0), "vT"], writes=[(vn, 0)])
                    S.act(lambda e, ub_=ub_: e.activation(out=vg[ub_][:, 1:1 + W], in_=vg[ub_][:, 1:1 + W], func=AF.Silu), reads=[("vg", 0)], writes=[("vg", 0)])
                    S.dve(lambda e, ub_=ub_: e.tensor_tensor(out=ao[ub_][:, 0:NLAT], in0=va[ub_][:, 1:1 + NLAT], in1=vg[ub_][:, 1:1 + NLAT], op=ALU.mult),
                          reads=[("va", 0), ("vg", 0)], writes=[("ao", ub_)])
                    c0 = 1 + NLAT + 2
                    S.dve(lambda e, ub_=ub_, c0=c0: e.tensor_tensor(out=ao[ub_][:, NLAT:T], in0=va[ub_][:, c0:c0 + NCTX], in1=vg[ub_][:, c0:c0 + NCTX], op=ALU.mult),
                          reads=[("va", 0), ("vg", 0), ("ao", ub_)], writes=[("ao", ub_)])
                    S.dma("sp", lambda e, f=f, ub_=ub_: e.dma_start(out=actT_d[f], in_=ao[ub_][:]), reads=[("ao", ub_)], writes=[("srcd", f)])
                if do_ada:
                    assert ada_j[0] == 96
                    ada_finish(l + 1, 7)
                end_phase()

        eps_t = gsb("eps_t", [128, 1], F32)
        S.dve(lambda e: e.memset(eps_t[:], EPS), writes=["eps"])
        prologue()
        phases = []
        for l in range(L):
            phases += [("n1", l), ("conv", l), ("pool", l), ("mla", l), ("gqa", l), ("merge", l), ("wout", l), ("n2", l), ("up", l), ("down", l)]
        for (ph, l) in phases:
            if done["stop"]:
                break
            if ph == "n1":
                if l == 0:
                    norm_phase(l, 0, have_stats=False)
            elif ph == "conv":
                conv_phase(l, fused_norm=(l > 0))
            elif ph == "pool":
                pool_phase(l)
            elif ph == "mla":
                mla_phase(l)
            elif ph == "gqa":
                gqa_phase(l)
            elif ph == "merge":
                merge_phase(l)
            elif ph == "wout":
                resid_phase(l, 0)
            elif ph == "n2":
                pass
            elif ph == "up":
                ffn_up_phase(l)
            elif ph == "down":
                resid_phase(l, 1)
            check_stop((ph, l))
        norm_phase(L - 1, 2, have_stats=True)
    return nc


_CACHE = {}


def kernel(**inputs):
    inp = {k: np.asarray(v) for k, v in inputs.items()}
    consts = _make_consts()
    if "nc" not in _CACHE:
        _CACHE["nc"] = build_program()
    nc = _CACHE["nc"]
    shared = {
        "sink": np.ascontiguousarray(inp["gqa_sink"], dtype=np.float32).reshape(-1),
        "w_ada": inp["w_ada"], "w_in": inp["w_in"], "mla_w_uq": inp["mla_w_uq"], "mla_w_ukv": inp["mla_w_ukv"],
        "pool_w": inp["pool_w"], "w_branch": inp["w_branch"], "w_out": inp["w_out"],
        "ffn_w_up": inp["ffn_w_up"], "ffn_w_down": inp["ffn_w_down"],
    }
    shared.update(consts)
    shared = {k: np.ascontiguousarray(v, dtype=np.float32) for k, v in shared.items()}
    in_maps = []
    for b in range(8):
        m = dict(shared)
        m["x"] = np.ascontiguousarray(inp["x"][b], dtype=np.float32)
        m["ctx"] = np.ascontiguousarray(inp["ctx"][b], dtype=np.float32)
        m["vecs"] = _make_vecs(inp, b)
        in_maps.append(m)
    res = run_bass_kernel_spmd(nc, in_maps, core_ids=list(range(8)))
    if DEBUG["on"]:
        _CACHE["last"] = res
    out = np.stack([np.asarray(r["out"], dtype=np.float32) for r in res.results], 0)
    return out
```

```python
import contextlib
import numpy as np
import concourse.bass as bass
import concourse.mybir as mybir
from concourse.bass_utils import run_bass_kernel_spmd

F32 = mybir.dt.float32
BF16 = mybir.dt.bfloat16
AF = mybir.ActivationFunctionType
ALU = mybir.AluOpType

L = 4
D = 2048
KC = 16
T = 2304
NLAT = 2048
NCTX = 256
TCH = [(0, 512), (512, 512), (1024, 512), (1536, 512), (2048, 256)]
IN_COLS = 11584
DFF = 5504
FCH = 43
EPS = 1e-6
N_DMA_SEMS = 24

DEBUG = {"on": False, "stop": None}


class Sched:
    COMPUTE = ("pe", "act", "dve", "pool")
    ALLENG = ("pe", "act", "dve", "pool", "sp")
    MAP = {"pe": "tensor", "act": "scalar", "dve": "vector", "pool": "gpsimd", "sp": "sync"}

    def __init__(self, nc, stack):
        self.nc = nc
        self.ops = []
        self.last_writer = {}
        self.readers = {}
        self.n_dma = 0
        self.emitted = 0
        self.cnt = {e: 0 for e in self.COMPUTE}
        self.waited = {e: {} for e in self.ALLENG}
        self.sems = {}
        for e in self.COMPUTE:
            self.sems[("c", e)] = stack.enter_context(nc.semaphore("s_" + e))
        for i in range(N_DMA_SEMS):
            self.sems[("dma", i)] = stack.enter_context(nc.semaphore("s_dma%d" % i))
        self.last_on = {}
        self.dma_hist = []

    def add(self, eng, fn, reads=(), writes=(), dma=False):
        idx = len(self.ops)
        deps = set()
        for r in reads:
            w = self.last_writer.get(r)
            if w is not None:
                deps.add(w)
        for w_ in writes:
            w = self.last_writer.get(w_)
            if w is not None:
                deps.add(w)
            for rd in self.readers.get(w_, ()):
                deps.add(rd)
        deps.discard(idx)
        op = dict(eng=eng, fn=fn, deps=deps, dma=dma, signal=dma, idx=idx)
        if dma:
            op["dma_i"] = self.n_dma
            self.n_dma += 1
            self.dma_hist.append(idx)
        self.ops.append(op)
        for r in reads:
            self.readers.setdefault(r, []).append(idx)
        for w_ in writes:
            self.last_writer[w_] = idx
            self.readers[w_] = []
        if fn is not None:
            self.last_on[eng] = idx
        return idx

    def pe(self, fn, reads=(), writes=()):
        return self.add("pe", fn, reads, writes)

    def act(self, fn, reads=(), writes=()):
        return self.add("act", fn, reads, writes)

    def dve(self, fn, reads=(), writes=()):
        return self.add("dve", fn, reads, writes)

    def dma(self, q, fn, reads=(), writes=()):
        return self.add(q, fn, reads, writes, dma=True)

    def barrier(self):
        deps = set(self.last_on.values()) | set(self.dma_hist[-N_DMA_SEMS:])
        for e in self.ALLENG:
            idx = self.add(e, None)
            self.ops[idx]["deps"] = set(d for d in deps if d >= self.emitted)
        self.last_writer = {}
        self.readers = {}

    def emit(self, final=False):
        nc = self.nc
        ops = self.ops
        start = self.emitted
        new = ops[start:]
        for op in new:
            op["deps"] = set(d for d in op["deps"] if d >= start)
            for d in op["deps"]:
                ops[d]["signal"] = True
        for op in new:
            if op["dma"]:
                i = op["dma_i"]
                op["sem"] = ("dma", i % N_DMA_SEMS)
                op["val"] = 16 * (i // N_DMA_SEMS + 1)
            elif op["signal"]:
                self.cnt[op["eng"]] += 1
                op["sem"] = ("c", op["eng"])
                op["val"] = self.cnt[op["eng"]]
        sems = self.sems

        def emit_engine(ename, e):
            waited = self.waited[ename]
            my = [op for op in new if op["eng"] == ename]
            for op in my:
                need = {}
                for d in op["deps"]:
                    dop = ops[d]
                    if dop["eng"] == "pe" and ename == "pe" and not dop["dma"]:
                        continue
                    k = dop["sem"]
                    need[k] = max(need.get(k, 0), dop["val"])
                if op["dma"]:
                    i = op["dma_i"]
                    if i >= N_DMA_SEMS:
                        k = ("dma", i % N_DMA_SEMS)
                        need[k] = max(need.get(k, 0), 16 * (i // N_DMA_SEMS))
                for k, v in sorted(need.items()):
                    if waited.get(k, 0) >= v:
                        continue
                    e.wait_ge(sems[k], v)
                    waited[k] = v
                if op["fn"] is None:
                    continue
                ins = op["fn"](e)
                if op["signal"]:
                    ins.then_inc(sems[op["sem"]], 16 if op["dma"] else 1)
            if final:
                fin = {}
                for op in ops:
                    if op["dma"]:
                        fin[op["sem"]] = max(fin.get(op["sem"], 0), op["val"])
                for k, v in sorted(fin.items()):
                    if waited.get(k, 0) >= v:
                        continue
                    e.wait_ge(sems[k], v)
                    waited[k] = v

        with nc.Block() as block:
            for ename in self.ALLENG:
                def mk(ename):
                    def f(e):
                        emit_engine(ename, e)
                    return f
                getattr(block, self.MAP[ename])(mk(ename))
        self.emitted = len(ops)


VEC_BLOCKS = [("n1g", L * 16), ("n2g", L * 16), ("fng", 16), ("qn", L * 4), ("kvn", L * 4),
              ("cb", L * 4), ("lng", L * 4), ("lnb", L * 4), ("psc", L * 4),
              ("cw", L * 124), ("fcw", L * 258), ("bada", L * 96), ("cc", 32)]
VEC_BASE = {}
_r = 0
for _n, _c in VEC_BLOCKS:
    VEC_BASE[_n] = _r
    _r += _c
VEC_ROWS = ((_r + 127) // 128) * 128
VEC_TILES = VEC_ROWS // 128

GLW = 2364
PLW = 2336
FFW = 2308


def _make_consts():
    ident = np.eye(128, dtype=np.float32)
    ones = np.ones((128, 128), np.float32)
    perm = np.zeros((128, 128), np.float32)
    for o in range(128):
        jj = o % 32
        p = o + 16 if jj < 16 else o - 16
        perm[p, o] = 1.0
    mask = np.zeros((128, 384), np.float32)
    kk = np.arange(128)[:, None]
    qq = np.arange(128)[None, :]
    mask[:, 0:128] = np.where(kk <= qq, 0.0, -30000.0)
    mask[:, 256:384] = np.where(qq <= kk, 0.0, -30000.0)
    half = 32
    inv_freq = (np.float32(10000.0) ** (-np.arange(0, half, 2, dtype=np.float32) / np.float32(half))).astype(np.float32)
    t = np.arange(NLAT)
    row = (t // 64).astype(np.float32)
    col = (t % 64).astype(np.float32)
    cosT = np.zeros((128, NLAT), np.float32)
    sinT = np.zeros((128, NLAT), np.float32)
    for p in range(128):
        j = p % 64
        pos = row if j < 32 else col
        jj = j % 32
        fi = jj % 16
        ang = (pos * inv_freq[fi]).astype(np.float32)
        cosT[p] = np.cos(ang)
        s = np.sin(ang)
        sinT[p] = -s if jj < 16 else s
    rc = np.zeros((4, PLW), np.float32)
    for gi, w in enumerate((2, 4, 8, 16)):
        for (l, off) in ((NLAT, 8), (NCTX, 8 + NLAT + 16)):
            tt = np.arange(l)
            lo = np.clip(tt - w // 2, 0, l)
            hi = np.clip(tt - w // 2 + w, 0, l)
            rc[gi, off:off + l] = 1.0 / (hi - lo).astype(np.float32)
    return dict(c_ident=ident, c_ones=ones, c_perm=perm, c_mask=mask, c_cos=cosT, c_sin=sinT, c_rc=rc)


def _make_vecs(inp, b):
    rows = np.zeros((VEC_ROWS, 128), np.float32)

    def put(name, arr):
        a = np.ascontiguousarray(arr, dtype=np.float32).reshape(-1, 128)
        rows[VEC_BASE[name]:VEC_BASE[name] + a.shape[0]] = a

    put("n1g", inp["norm1_g"])
    put("n2g", inp["norm2_g"])
    put("fng", inp["final_norm_g"])
    put("qn", inp["mla_q_norm"])
    put("kvn", inp["mla_kv_norm"])
    put("cb", inp["conv_b"])
    put("lng", inp["conv_ln_g"])
    put("lnb", inp["conv_ln_b"])
    put("psc", inp["pool_scale"])
    put("cw", inp["conv_w"])
    put("fcw", inp["ffn_conv_w"])
    put("bada", inp["b_ada"])
    put("cc", np.stack([inp["c"][b], inp["c_ctx"]], 0))
    return rows


def build_program():
    nc = bass.Bass("TRN2", target_bir_lowering=False)
    dbg = DEBUG["on"]
    stop = DEBUG["stop"]

    def din(name, shape):
        return nc.dram_tensor(name, list(shape), F32, kind="ExternalInput").ap()

    x_d = din("x", [NLAT, D])
    ctx_d = din("ctx", [NCTX, D])
    vecs_d = din("vecs", [VEC_ROWS, 128])
    sink_d = din("sink", [L * 8])
    w_ada_d = din("w_ada", [L, D, 6 * D])
    w_in_d = din("w_in", [L, D, IN_COLS])
    w_uq_d = din("mla_w_uq", [L, 512, 768])
    w_ukv_d = din("mla_w_ukv", [L, 512, 1024])
    pool_w_d = din("pool_w", [L, 4, 128, 128])
    w_br_d = din("w_branch", [L, 4, 512, D])
    w_out_d = din("w_out", [L, D, D])
    w_up_d = din("ffn_w_up", [L, D, 2 * DFF])
    w_dn_d = din("ffn_w_down", [L, DFF, D])
    c_ident_d = din("c_ident", [128, 128])
    c_ones_d = din("c_ones", [128, 128])
    c_perm_d = din("c_perm", [128, 128])
    c_mask_d = din("c_mask", [128, 384])
    c_cos_d = din("c_cos", [128, NLAT])
    c_sin_d = din("c_sin", [128, NLAT])
    c_rc_d = din("c_rc", [4, PLW])
    out_d = nc.dram_tensor("out", [NLAT, D], F32, kind="ExternalOutput").ap()

    skind = "ExternalOutput" if dbg else "Internal"
    xT_d = nc.dram_tensor("s_xT", [KC, 128, T], F32, kind=skind).ap()
    brT_d = nc.dram_tensor("s_brT", [16, 128, T], BF16, kind=skind).ap()
    yT_d = nc.dram_tensor("s_yT", [KC, 128, T], BF16, kind=skind).ap()
    actT_d = nc.dram_tensor("s_actT", [FCH, 128, T], BF16, kind=skind).ap()
    rs_d = nc.dram_tensor("s_rs", [128, T], F32, kind="Internal").ap()
    hxT_d = nc.dram_tensor("s_hxT", [KC, 128, T], BF16, kind=skind).ap() if dbg else None

    _uc = [0]

    def un(name):
        _uc[0] += 1
        return "%s_%d" % (name, _uc[0])

    with contextlib.ExitStack() as gst:
        S = Sched(nc, gst)

        def gsb(name, shape, dt):
            return gst.enter_context(nc.sbuf_tensor(un(name), list(shape), dt))

        PS = gst.enter_context(nc.psum_tensor("PS", [128, 8, 512], F32))
        ident_f = gsb("ident_f", [128, 128], F32)
        ident_b = gsb("ident_b", [128, 128], BF16)
        ones_b = gsb("ones_b", [128, 128], BF16)
        perm_b = gsb("perm_b", [128, 128], BF16)
        mask_b = gsb("mask_b", [128, 384], BF16)
        vT = gsb("vT", [128, VEC_ROWS], F32)
        modT = gsb("modT", [128, L, 6, 16, 2], F32)
        geff = gsb("geff", [128, L, 2, 16, 2], F32)
        esT = gsb("esT", [128, L, 4], F32)
        A = gsb("A", [128, KC, T], BF16)
        silb = gsb("silb", [128, 2, 16], BF16)

        def vcol(name, idx):
            c = VEC_BASE[name] + idx
            return vT[:, c:c + 1]

        def psb(b):
            return PS[:, b, :]

        def pid(b):
            return ("ps", b)

        def mm_group(out_ap, pairs, reads, writes):
            def fn(e, pairs=pairs, out_ap=out_ap):
                n = len(pairs)
                ins = None
                for i, (a, b) in enumerate(pairs):
                    ins = e.matmul(out_ap, a, b, start=(i == 0), stop=(i == n - 1))
                return ins
            S.pe(fn, reads, writes)

        def end_phase(final=False):
            S.barrier()
            S.emit(final=final)

        done = {"stop": False}

        def check_stop(tag):
            if stop is not None and stop == tag:
                done["stop"] = True
            return done["stop"]

        def ada_finish(l, bank):
            bb = VEC_BASE["bada"] + l * 96
            def evmod(e):
                return e.tensor_tensor(
                    out=modT[:, l].rearrange("p m k s -> p (m k) s"),
                    in0=PS[:, bank, 0:192].rearrange("p (j s) -> p j s", s=2),
                    in1=vT[:, bb:bb + 96].unsqueeze(2).to_broadcast([128, 96, 2]), op=ALU.add)
            S.dve(evmod, reads=[pid(bank)], writes=[("modT", l)])
            for which, (gname, mi) in enumerate((("n1g", 1), ("n2g", 4))):
                gb = VEC_BASE[gname] + l * 16
                def evg(e, which=which, mi=mi, gb=gb):
                    return e.scalar_tensor_tensor(
                        out=geff[:, l, which], in0=modT[:, l, mi], scalar=1.0,
                        in1=vT[:, gb:gb + 16].unsqueeze(2).to_broadcast([128, 16, 2]),
                        op0=ALU.add, op1=ALU.mult)
                S.dve(evg, reads=[("modT", l)], writes=[("geff", l, which)])

        def prologue():
            with contextlib.ExitStack() as st:
                def sb(name, shape, dt):
                    return st.enter_context(nc.sbuf_tensor(un(name), list(shape), dt))
                S.dma("sp", lambda e: e.dma_start(out=ident_f[:], in_=c_ident_d[:, :]), writes=["ident_f"])
                S.dma("pool", lambda e: e.dma_start(out=ident_b[:], in_=c_ident_d[:, :]), writes=["ident_b"])
                S.dma("pool", lambda e: e.dma_start(out=ones_b[:], in_=c_ones_d[:, :]), writes=["ones_b"])
                S.dma("pool", lambda e: e.dma_start(out=perm_b[:], in_=c_perm_d[:, :]), writes=["perm_b"])
                S.dma("pool", lambda e: e.dma_start(out=mask_b[:], in_=c_mask_d[:, :]), writes=["mask_b"])
                vin = [sb("vin%d" % i, [128, 128], F32) for i in range(2)]
                for vt in range(VEC_TILES):
                    b = vt % 2
                    S.dma("sp", lambda e, vt=vt, b=b: e.dma_start(out=vin[b][:], in_=vecs_d[vt * 128:(vt + 1) * 128, :]),
                          writes=[("vin", b)])
                    S.pe(lambda e, b=b: e.transpose(PS[:, b, 0:128], vin[b][:], ident_f[:]),
                         reads=[("vin", b), "ident_f"], writes=[pid(b)])
                    S.act(lambda e, vt=vt, b=b: e.activation(out=vT[:, vt * 128:(vt + 1) * 128], in_=PS[:, b, 0:128], func=AF.Copy),
                          reads=[pid(b)], writes=["vT"])
                sk = sb("sk", [128, L * 8], F32)
                skb = bass.AP(tensor=sink_d.tensor, offset=0, ap=[[0, 128], [1, L * 8]])
                S.dma("sp", lambda e: e.dma_start(out=sk[:], in_=skb), writes=["sk"])
                S.act(lambda e: e.activation(out=sk[:], in_=sk[:], func=AF.Exp), reads=["sk"], writes=["sk"])
                skv = sk[:].rearrange("p (l h two) -> p l h two", l=L, two=2)
                S.dve(lambda e: e.tensor_copy(out=esT[0:64, :, :], in_=skv[0:64, :, :, 0]), reads=["sk"], writes=["esT0"])
                S.dve(lambda e: e.tensor_copy(out=esT[64:128, :, :], in_=skv[64:128, :, :, 1]), reads=["sk"], writes=["esT1"])
                xin = [sb("xin%d" % i, [128, D], F32) for i in range(2)]
                xst = [sb("xst%d" % i, [128, KC, 128], F32) for i in range(2)]
                xTv = xT_d.rearrange("k p t -> p k t")
                for tt in range(18):
                    b = tt % 2
                    src = x_d[tt * 128:(tt + 1) * 128, :] if tt < 16 else ctx_d[(tt - 16) * 128:(tt - 15) * 128, :]
                    S.dma("sp", lambda e, b=b, src=src: e.dma_start(out=xin[b][:], in_=src), writes=[("xin", b)])
                    for q in range(4):
                        bank = 2 + (tt * 4 + q) % 6
                        def tp(e, b=b, q=q, bank=bank):
                            ins = None
                            for j in range(4):
                                k = q * 4 + j
                                ins = e.transpose(PS[:, bank, j * 128:(j + 1) * 128], xin[b][:, k * 128:(k + 1) * 128], ident_f[:])
                            return ins
                        S.pe(tp, reads=[("xin", b), "ident_f"], writes=[pid(bank)])
                        eng = S.act if q % 2 == 0 else S.dve
                        if q % 2 == 0:
                            S.act(lambda e, b=b, q=q, bank=bank: e.activation(
                                out=xst[b][:, q * 4:(q + 1) * 4, :], in_=PS[:, bank, :].rearrange("p (j t) -> p j t", j=4), func=AF.Copy),
                                reads=[pid(bank)], writes=[("xst", b, q)])
                        else:
                            S.dve(lambda e, b=b, q=q, bank=bank: e.tensor_copy(
                                out=xst[b][:, q * 4:(q + 1) * 4, :], in_=PS[:, bank, :].rearrange("p (j t) -> p j t", j=4)),
                                reads=[pid(bank)], writes=[("xst", b, q)])
                    S.dma("sp", lambda e, b=b, tt=tt: e.dma_start(out=xTv[:, :, tt * 128:(tt + 1) * 128], in_=xst[b][:]),
                          reads=[("xst", b, q) for q in range(4)], writes=["xT"])
                sil = sb("sil", [128, 32], F32)
                ccb = VEC_BASE["cc"]
                S.act(lambda e: e.activation(out=sil[:], in_=vT[:, ccb:ccb + 32], func=AF.Silu), reads=["vT"], writes=["sil"])
                S.dve(lambda e: e.tensor_copy(out=silb[:].rearrange("p s k -> p (s k)"), in_=sil[:]), reads=["sil"], writes=["silb"])
                wa = [sb("wa%d" % i, [128, KC, 512], BF16) for i in range(3)]
                wv = w_ada_d[0].rearrange("(k p) n -> p k n", p=128)
                for nb in range(24):
                    wi = nb % 3
                    S.dma("pool", lambda e, wi=wi, wv=wv, nb=nb: e.dma_start(out=wa[wi][:], in_=wv[:, :, nb * 512:(nb + 1) * 512]),
                          writes=[("wa", wi)])
                    def adamm(e, wi=wi, nb=nb):
                        ins = None
                        for jj in range(4):
                            j = nb * 4 + jj
                            for k in range(KC):
                                ins = e.matmul(PS[:, 0, j * 2:j * 2 + 2], wa[wi][:, k, jj * 128:(jj + 1) * 128],
                                               silb[:, :, k], start=(k == 0), stop=(k == KC - 1))
                        return ins
                    S.pe(adamm, reads=[("wa", wi), "silb"], writes=[pid(0)])
                ada_finish(0, 0)
                end_phase()

        def norm_phase(l, which, have_stats=False):
            final = which == 2
            with contextlib.ExitStack() as st:
                def sb(name, shape, dt):
                    return st.enter_context(nc.sbuf_tensor(un(name), list(shape), dt))
                NT = NLAT if final else T
                tch = [c for c in TCH if c[0] < NT]
                if not (final and have_stats):
                    xl = [sb("xl%d" % i, [128, NT], F32) for i in range(3)]
                    sq = [sb("sq%d" % i, [128, NT], BF16) for i in range(2)]
                rstd = sb("rstd", [128, NT], F32)
                nb = len(tch)
                if have_stats:
                    S.dma("sp", lambda e: e.dma_start(out=rstd[:, 0:NT], in_=rs_d[:, 0:NT]), writes=["rstdf"])
                else:
                    for k in range(KC):
                        b = k % 3
                        S.dma("sp", lambda e, k=k, b=b: e.dma_start(out=xl[b][:, 0:NT], in_=xT_d[k][:, 0:NT]), writes=[("xl", b)])
                        S.act(lambda e, k=k, b=b: e.activation(out=sq[k % 2][:, 0:NT], in_=xl[b][:, 0:NT], func=AF.Square),
                              reads=[("xl", b)], writes=[("sq", k % 2)])
                        def ssmm(e, k=k):
                            ins = None
                            for ci, (t0, n) in enumerate(tch):
                                ins = e.matmul(PS[:, ci, 0:n], ones_b[:], sq[k % 2][:, t0:t0 + n], start=(k == 0), stop=(k == KC - 1))
                            return ins
                        S.pe(ssmm, reads=[("sq", k % 2)], writes=[pid(ci) for ci in range(nb)])
                    for ci, (t0, n) in enumerate(tch):
                        S.act(lambda e, ci=ci, t0=t0, n=n: e.activation(out=rstd[:, t0:t0 + n], in_=PS[:, ci, 0:n], func=AF.Sqrt,
                                                                         scale=1.0 / D, bias=eps_t[:, 0:1]),
                              reads=[pid(ci)], writes=[("rstd", ci)])
                    S.dve(lambda e: e.reciprocal(out=rstd[:, 0:NT], in_=rstd[:, 0:NT]),
                          reads=[("rstd", ci) for ci in range(nb)], writes=["rstdf"])
                if not final:
                    tmp = [sb("ntmp%d" % i, [128, T], F32) for i in range(2)]
                    mi = 0 if which == 0 else 3
                    for k in range(KC):
                        b = k % 3
                        S.dma("sp", lambda e, k=k, b=b: e.dma_start(out=xl[b][:, 0:NT], in_=xT_d[k][:, 0:NT]), writes=[("xl", b)])
                        tb = k % 2
                        S.dve(lambda e, b=b, tb=tb: e.tensor_tensor(out=tmp[tb][:, 0:NT], in0=xl[b][:, 0:NT], in1=rstd[:, 0:NT], op=ALU.mult),
                              reads=[("xl", b), "rstdf"], writes=[("tmp", tb)])
                        for s, (t0, n) in enumerate(((0, NLAT), (NLAT, NCTX))):
                            S.act(lambda e, k=k, tb=tb, s=s, t0=t0, n=n: e.activation(
                                out=A[:, k, t0:t0 + n], in_=tmp[tb][:, t0:t0 + n], func=AF.Identity,
                                scale=geff[:, l, which, k, s:s + 1], bias=modT[:, l, mi, k, s:s + 1]),
                                reads=[("tmp", tb)], writes=[("A", k, s)])
                        if dbg and which == 0:
                            S.dma("sp", lambda e, k=k: e.dma_start(out=hxT_d[k], in_=A[:, k, :]),
                                  reads=[("A", k, 0), ("A", k, 1)], writes=[("hxTd", k)])
                else:
                    xs = [sb("xs%d" % i, [128, 8, 512], F32) for i in range(2)]
                    tmp = [sb("ntmp%d" % i, [128, 512], F32) for i in range(2)]
                    outb = [sb("outb%d" % i, [128, 4, D], F32) for i in range(2)]
                    xTv4 = xT_d.rearrange("k p t -> p k t")
                    it = 0
                    nld = 0
                    for q in range(4):
                        ob_ = q % 2
                        for k in range(KC):
                            if k % 8 == 0:
                                b = nld % 2
                                nld += 1
                                S.dma("sp", lambda e, k=k, b=b, q=q: e.dma_start(out=xs[b][:], in_=xTv4[:, k:k + 8, q * 512:(q + 1) * 512]), writes=[("xs", b)])
                            kk = k % 8
                            tb = it % 2
                            gcol = vcol("fng", k)
                            S.dve(lambda e, b=b, kk=kk, tb=tb, q=q, gcol=gcol: e.scalar_tensor_tensor(
                                out=tmp[tb][:], in0=xs[b][:, kk, :], scalar=gcol, in1=rstd[:, q * 512:(q + 1) * 512], op0=ALU.mult, op1=ALU.mult),
                                reads=[("xs", b), "rstdf"], writes=[("tmp", tb)])
                            bank = 4 + it % 4
                            def tp(e, tb=tb, bank=bank):
                                ins = None
                                for j in range(4):
                                    ins = e.transpose(PS[:, bank, j * 128:(j + 1) * 128], tmp[tb][:, j * 128:(j + 1) * 128], ident_f[:])
                                return ins
                            S.pe(tp, reads=[("tmp", tb)], writes=[pid(bank)])
                            dst = outb[ob_][:, :, k * 128:(k + 1) * 128]
                            src = PS[:, bank, :].rearrange("p (j c) -> p j c", j=4)
                            if it % 2 == 0:
                                S.act(lambda e, dst=dst, src=src: e.activation(out=dst, in_=src, func=AF.Copy),
                                      reads=[pid(bank)], writes=[("outb", ob_, k)])
                            else:
                                S.dve(lambda e, dst=dst, src=src: e.tensor_copy(out=dst, in_=src),
                                      reads=[pid(bank)], writes=[("outb", ob_, k)])
                            it += 1
                        for j in range(4):
                            tt = q * 4 + j
                            S.dma("sp", lambda e, tt=tt, j=j, ob_=ob_: e.dma_start(out=out_d[tt * 128:(tt + 1) * 128, :], in_=outb[ob_][:, j, :]),
                                  reads=[("outb", ob_, k) for k in range(KC)], writes=[("out", tt)])
                end_phase(final=final)

        def proj_block(wt, wid, c0, ncols, tc, bank, rid):
            t0, n = TCH[tc]
            pairs = [(wt[:, k, c0:c0 + ncols], A[:, k, t0:t0 + n]) for k in range(KC)]
            mm_group(PS[0:ncols, bank, 0:n], pairs, reads=[(wid, c0 // 128)] + rid, writes=[pid(bank)])

        def load_w(dst, srcview, col0, ncols, wid, only=None):
            for pc in range((ncols + 127) // 128):
                if only is not None and pc != only:
                    continue
                a = pc * 128
                b = min(ncols, a + 128)
                S.dma("pool", lambda e, a=a, b=b: e.dma_start(out=dst[:, :, a:b], in_=srcview[:, :, col0 + a:col0 + b]),
                      writes=[(wid, pc)])

        def conv_phase(l, fused_norm=False):
            with contextlib.ExitStack() as st:
                def sb(name, shape, dt):
                    return st.enter_context(nc.sbuf_tensor(un(name), list(shape), dt))
                glu = sb("glu", [128, 4, GLW], BF16)

                def gcol0(tc):
                    t0, n = TCH[tc]
                    return (15 + t0) if tc < 4 else (15 + NLAT + 30)

                with contextlib.ExitStack() as st1:
                    def sb1(name, shape, dt):
                        return st1.enter_context(nc.sbuf_tensor(un(name), list(shape), dt))
                    wv = w_in_d[l].rearrange("(k p) n -> p k n", p=128)
                    wA = sb1("wA", [128, KC, 512], BF16)
                    wG = sb1("wG", [128, KC, 512], BF16)
                    for pc in range(4):
                        load_w(wA, wv, 1856, 512, "wA", only=pc)
                        load_w(wG, wv, 2368, 512, "wG", only=pc)
                    S.dve(lambda e: e.memset(glu[:], 0.0), writes=["glu"])
                    sig = [sb1("sig%d" % i, [128, 512], F32) for i in range(2)]
                    if fused_norm:
                        xs = [sb1("xs%d" % i, [128, 4, 512], F32) for i in range(3)]
                        tn = [sb1("tn%d" % i, [128, 4, 512], F32) for i in range(2)]
                        rstd = sb1("rstd", [128, T], F32)
                        S.dma("sp", lambda e: e.dma_start(out=rstd[:], in_=rs_d[:, :]), writes=["rstdf"])
                        xTv4 = xT_d.rearrange("k p t -> p k t")
                        ni = 0
                        for tc in range(5):
                            t0, n = TCH[tc]
                            s = 1 if tc == 4 else 0
                            for kg in range(4):
                                b = ni % 3
                                tb = ni % 2
                                ni += 1
                                S.dma("sp", lambda e, b=b, kg=kg, t0=t0, n=n: e.dma_start(out=xs[b][:, :, 0:n], in_=xTv4[:, kg * 4:(kg + 1) * 4, t0:t0 + n]),
                                      writes=[("xs", b)])
                                S.dve(lambda e, b=b, tb=tb, t0=t0, n=n: e.tensor_tensor(
                                    out=tn[tb][:, :, 0:n], in0=xs[b][:, :, 0:n],
                                    in1=rstd[:, t0:t0 + n].unsqueeze(1).to_broadcast([128, 4, n]), op=ALU.mult),
                                    reads=[("xs", b), "rstdf"], writes=[("tn", tb)])
                                for j in range(4):
                                    k = kg * 4 + j
                                    S.act(lambda e, k=k, j=j, tb=tb, s=s, t0=t0, n=n: e.activation(
                                        out=A[:, k, t0:t0 + n], in_=tn[tb][:, j, 0:n], func=AF.Identity,
                                        scale=geff[:, l, 0, k, s:s + 1], bias=modT[:, l, 0, k, s:s + 1]),
                                        reads=[("tn", tb)], writes=[("A", k, tc)])
                    i = 0
                    for c in range(4):
                        for tc in range(5):
                            t0, n = TCH[tc]
                            ba, bg = (i % 4) * 2, (i % 4) * 2 + 1
                            arid = [("A", k, tc) for k in range(KC)] if fused_norm else []
                            proj_block(wA, "wA", c * 128, 128, tc, ba, arid)
                            proj_block(wG, "wG", c * 128, 128, tc, bg, arid)
                            sb_ = i % 2
                            S.act(lambda e, bg=bg, n=n, sb_=sb_: e.activation(out=sig[sb_][:, 0:n], in_=PS[:, bg, 0:n], func=AF.Sigmoid),
                                  reads=[pid(bg)], writes=[("sig", sb_)])
                            g0 = gcol0(tc)
                            S.dve(lambda e, ba=ba, n=n, sb_=sb_, c=c, g0=g0: e.tensor_tensor(
                                out=glu[:, c, g0:g0 + n], in0=sig[sb_][:, 0:n], in1=PS[:, ba, 0:n], op=ALU.mult),
                                reads=[pid(ba), ("sig", sb_), "glu"], writes=[("glu", c, tc)])
                            i += 1
                    end_phase()
                dg = sb("dg", [128, 4, 31, 128], BF16)
                for c in range(4):
                    for k in range(31):
                        col = vcol("cw", l * 124 + k * 4 + c)
                        S.dve(lambda e, c=c, k=k, col=col: e.tensor_scalar(out=dg[:, c, k, :], in0=ident_f[:], scalar1=col, scalar2=None, op0=ALU.mult),
                              writes=[("dg", c)])
                obr = sb("obr", [128, 4, T], BF16)
                uf = sb("uf", [128, 4, 512], F32)
                ub = sb("ub", [128, 4, 512], BF16)
                usq = sb("usq", [128, 4, 512], BF16)
                mean = sb("mean", [128, 512], F32)
                var = sb("var", [128, 512], F32)
                xc = [sb("xc%d" % i, [128, 512], F32) for i in range(2)]
                for tc in range(5):
                    t0, n = TCH[tc]
                    g0 = gcol0(tc) - 15
                    for c in range(4):
                        bank = 4 + c
                        pairs = [(dg[:, c, k, :], glu[:, c, g0 + k:g0 + k + n]) for k in range(31)]
                        mm_group(PS[:, bank, 0:n], pairs, reads=[("dg", c)], writes=[pid(bank)])
                        cb = vcol("cb", l * 4 + c)
                        S.act(lambda e, c=c, bank=bank, n=n, cb=cb: e.activation(out=uf[:, c, 0:n], in_=PS[:, bank, 0:n], func=AF.Identity, bias=cb),
                              reads=[pid(bank)], writes=[("uf", c)])
                        S.dve(lambda e, c=c, n=n: e.tensor_copy(out=ub[:, c, 0:n], in_=uf[:, c, 0:n]), reads=[("uf", c)], writes=[("ub", c)])
                        S.act(lambda e, c=c, n=n: e.activation(out=usq[:, c, 0:n], in_=uf[:, c, 0:n], func=AF.Square), reads=[("uf", c)], writes=[("usq", c)])
                    mm_group(PS[:, 0, 0:n], [(ones_b[:], ub[:, c, 0:n]) for c in range(4)], reads=[("ub", c) for c in range(4)], writes=[pid(0)])
                    mm_group(PS[:, 1, 0:n], [(ones_b[:], usq[:, c, 0:n]) for c in range(4)], reads=[("usq", c) for c in range(4)], writes=[pid(1)])
                    S.dve(lambda e, n=n: e.tensor_scalar(out=mean[:, 0:n], in0=PS[:, 0, 0:n], scalar1=1.0 / 512, scalar2=None, op0=ALU.mult),
                          reads=[pid(0)], writes=["mean"])
                    S.dve(lambda e, n=n: e.tensor_tensor(out=var[:, 0:n], in0=mean[:, 0:n], in1=mean[:, 0:n], op=ALU.mult), reads=["mean"], writes=["var"])
                    S.dve(lambda e, n=n: e.scalar_tensor_tensor(out=var[:, 0:n], in0=PS[:, 1, 0:n], scalar=1.0 / 512, in1=var[:, 0:n],
                                                                op0=ALU.mult, op1=ALU.subtract), reads=[pid(1), "var"], writes=["var"])
                    S.act(lambda e, n=n: e.activation(out=var[:, 0:n], in_=var[:, 0:n], func=AF.Sqrt, bias=eps_t[:, 0:1]), reads=["var"], writes=["var"])
                    S.dve(lambda e, n=n: e.reciprocal(out=var[:, 0:n], in_=var[:, 0:n]), reads=["var"], writes=["var"])
                    for c in range(4):
                        xb_ = c % 2
                        S.dve(lambda e, c=c, n=n, xb_=xb_: e.tensor_tensor(out=xc[xb_][:, 0:n], in0=uf[:, c, 0:n], in1=mean[:, 0:n], op=ALU.subtract),
                              reads=[("uf", c), "mean"], writes=[("xc", xb_)])
                        S.dve(lambda e, n=n, xb_=xb_: e.tensor_tensor(out=xc[xb_][:, 0:n], in0=xc[xb_][:, 0:n], in1=var[:, 0:n], op=ALU.mult),
                              reads=[("xc", xb_), "var"], writes=[("xc", xb_)])
                        lg, lb = vcol("lng", l * 4 + c), vcol("lnb", l * 4 + c)
                        S.act(lambda e, c=c, n=n, xb_=xb_, lg=lg, lb=lb, t0=t0: e.activation(
                            out=obr[:, c, t0:t0 + n], in_=xc[xb_][:, 0:n], func=AF.Silu, scale=lg, bias=lb),
                            reads=[("xc", xb_)], writes=[("obr", c)])
                for c in range(4):
                    S.dma("sp", lambda e, c=c: e.dma_start(out=brT_d[0 * 4 + c], in_=obr[:, c, :]), reads=[("obr", c)], writes=[("brT", c)])
                end_phase()

        def pool_phase(l):
            with contextlib.ExitStack() as st:
                def sb(name, shape, dt):
                    return st.enter_context(nc.sbuf_tensor(un(name), list(shape), dt))
                wv = w_in_d[l].rearrange("(k p) n -> p k n", p=128)
                wP = sb("wP", [128, KC, 512], BF16)
                load_w(wP, wv, 2880, 512, "wP")
                wpl = sb("wpl", [128, 4, 128], BF16)
                S.dma("pool", lambda e: e.dma_start(out=wpl[:], in_=pool_w_d[l].rearrange("g c d -> c g d")), writes=["wpl"])
                up = [sb("up%d" % i, [128, PLW], F32) for i in range(2)]
                s1 = sb("s1", [128, PLW], F32)
                s2 = sb("s2", [128, PLW], F32)
                rc = [sb("rc%d" % i, [128, PLW], F32) for i in range(2)]
                pp = [sb("pp%d" % i, [128, PLW], BF16) for i in range(2)]
                obr = sb("obr", [128, 4, T], BF16)
                for i in range(2):
                    S.dve(lambda e, i=i: e.memset(up[i][:], 0.0), writes=[("up", i)])

                def pcol0(tc):
                    t0, n = TCH[tc]
                    return (8 + t0) if tc < 4 else (8 + NLAT + 16)

                for g, w in enumerate((2, 4, 8, 16)):
                    ub_ = g % 2
                    rcb = bass.AP(tensor=c_rc_d.tensor, offset=g * PLW, ap=[[0, 128], [1, PLW]])
                    S.dma("sp", lambda e, ub_=ub_, rcb=rcb: e.dma_start(out=rc[ub_][:], in_=rcb), writes=[("rc", ub_)])
                    for tc in range(5):
                        t0, n = TCH[tc]
                        bank = (g * 5 + tc) % 4
                        proj_block(wP, "wP", g * 128, 128, tc, bank, [])
                        p0 = pcol0(tc)
                        S.act(lambda e, ub_=ub_, bank=bank, n=n, p0=p0: e.activation(out=up[ub_][:, p0:p0 + n], in_=PS[:, bank, 0:n], func=AF.Copy),
                              reads=[pid(bank)], writes=[("up", ub_)])
                    src = up[ub_]
                    width = 1
                    bufs = [s1, s2]
                    bi = 0
                    W = PLW
                    while width < w:
                        dst = bufs[bi]
                        nvalid = W - 2 * width + 1
                        S.dve(lambda e, src=src, dst=dst, width=width, nvalid=nvalid: e.tensor_tensor(
                            out=dst[:, 0:nvalid], in0=src[:, 0:nvalid], in1=src[:, width:width + nvalid], op=ALU.add),
                            reads=[("up", ub_), "s1", "s2"], writes=["s1" if bi == 0 else "s2"])
                        src = dst
                        width *= 2
                        bi ^= 1
                    h = w // 2
                    lo, hi = 8, PLW - 8
                    dstw = bufs[bi]
                    S.dve(lambda e, src=src, dstw=dstw, ub_=ub_, h=h, lo=lo, hi=hi: e.tensor_tensor(
                        out=dstw[:, lo:hi], in0=src[:, lo - h:hi - h], in1=rc[ub_][:, lo:hi], op=ALU.mult),
                        reads=["s1", "s2", ("rc", ub_)], writes=["s1" if bi == 0 else "s2"])
                    S.dve(lambda e, dstw=dstw, ub_=ub_, lo=lo, hi=hi: e.tensor_tensor(
                        out=pp[ub_][:, lo:hi], in0=dstw[:, lo:hi], in1=up[ub_][:, lo:hi], op=ALU.subtract),
                        reads=["s1", "s2", ("up", ub_)], writes=[("pp", ub_)])
                    for tc in range(5):
                        t0, n = TCH[tc]
                        bank = 4 + (g * 5 + tc) % 4
                        p0 = pcol0(tc)
                        mm_group(PS[:, bank, 0:n], [(wpl[:, g, :], pp[ub_][:, p0:p0 + n])], reads=["wpl", ("pp", ub_)], writes=[pid(bank)])
                        sc = vcol("psc", l * 4 + g)
                        S.act(lambda e, g=g, bank=bank, n=n, t0=t0, sc=sc: e.activation(out=obr[:, g, t0:t0 + n], in_=PS[:, bank, 0:n], func=AF.Copy, scale=sc),
                              reads=[pid(bank), "vT"], writes=[("obr", g)])
                for c in range(4):
                    S.dma("sp", lambda e, c=c: e.dma_start(out=brT_d[3 * 4 + c], in_=obr[:, c, :]), reads=[("obr", c)], writes=[("brT", c)])
                end_phase()

        def proj_rmsnorm(wt, wid, gname, l, dst, dstid, raw, sqt, rs):
            for tc in range(5):
                t0, n = TCH[tc]
                for c in range(4):
                    bank = c % 2
                    proj_block(wt, wid, c * 128, 128, tc, bank, [])
                    S.act(lambda e, c=c, bank=bank, n=n, t0=t0: e.activation(out=raw[:, c, 0:n], in_=PS[:, bank, 0:n], func=AF.Copy),
                          reads=[pid(bank)], writes=[("raw", c)])
                    S.act(lambda e, c=c, bank=bank, n=n: e.activation(out=sqt[c % 2][:, 0:n], in_=PS[:, bank, 0:n], func=AF.Square),
                          reads=[pid(bank)], writes=[("sqt", c % 2)])
                    S.pe(lambda e, c=c, n=n: e.matmul(PS[:, 2, 0:n], ones_b[:], sqt[c % 2][:, 0:n], start=(c == 0), stop=(c == 3)),
                         reads=[("sqt", c % 2), "ones_b"], writes=[pid(2)])
                S.act(lambda e, n=n: e.activation(out=rs[:, 0:n], in_=PS[:, 2, 0:n], func=AF.Sqrt, scale=1.0 / 512, bias=eps_t[:, 0:1]),
                      reads=[pid(2), "eps"], writes=["rs"])
                S.dve(lambda e, n=n: e.reciprocal(out=rs[:, 0:n], in_=rs[:, 0:n]), reads=["rs"], writes=["rs"])
                for c in range(4):
                    gc = vcol(gname, l * 4 + c)
                    S.dve(lambda e, c=c, n=n, t0=t0, gc=gc: e.scalar_tensor_tensor(
                        out=dst[:, c, t0:t0 + n], in0=raw[:, c, 0:n], scalar=gc, in1=rs[:, 0:n], op0=ALU.mult, op1=ALU.mult),
                        reads=[("raw", c), "rs", "vT"], writes=[(dstid, c, tc)])

        def rope_chunk(raw_ap, rawids, dst_ap, dstids, t0, n, cosb, sinb, rt, bank, slot):
            mm_group(PS[:, bank, 0:n], [(perm_b[:], raw_ap)], reads=["perm_b"] + rawids, writes=[pid(bank)])
            S.dve(lambda e: e.tensor_tensor(out=rt[slot][:, 0:n], in0=sinb[:, t0:t0 + n], in1=PS[:, bank, 0:n], op=ALU.mult),
                  reads=[pid(bank), "sin"], writes=[("rt", slot)])
            S.dve(lambda e: e.tensor_tensor(out=rt[2 + slot][:, 0:n], in0=raw_ap, in1=cosb[:, t0:t0 + n], op=ALU.mult),
                  reads=rawids + ["cos"], writes=[("rt", 2 + slot)])
            S.dve(lambda e: e.tensor_tensor(out=dst_ap, in0=rt[slot][:, 0:n], in1=rt[2 + slot][:, 0:n], op=ALU.add),
                  reads=[("rt", slot), ("rt", 2 + slot)], writes=dstids)

        def load_rope_tables(sb):
            cosb = sb("cosb", [128, NLAT], BF16)
            sinb = sb("sinb", [128, NLAT], BF16)
            S.dma("pool", lambda e: e.dma_start(out=cosb[:], in_=c_cos_d[:, :]), writes=["cos"])
            S.dma("pool", lambda e: e.dma_start(out=sinb[:], in_=c_sin_d[:, :]), writes=["sin"])
            return cosb, sinb

        def mla_phase(l):
            with contextlib.ExitStack() as st:
                def sb(name, shape, dt):
                    return st.enter_context(nc.sbuf_tensor(un(name), list(shape), dt))
                wv = w_in_d[l].rearrange("(k p) n -> p k n", p=128)
                knope = sb("knope", [128, 4, T], BF16)
                vtok = sb("vtok", [128, 18, 512], BF16)
                kr = sb("kr", [128, T], BF16)
                with contextlib.ExitStack() as st1:
                    def sb1(name, shape, dt):
                        return st1.enter_context(nc.sbuf_tensor(un(name), list(shape), dt))
                    wC = sb1("wC", [128, KC, 512], BF16)
                    wKR = sb1("wKR", [128, KC, 64], BF16)
                    wUKV = sb1("wUKV", [128, 4, 1024], BF16)
                    load_w(wC, wv, 0, 512, "wC")
                    S.dma("pool", lambda e: e.dma_start(out=wKR[:], in_=wv[:, :, 512:576]), writes=["wKR"])
                    S.dma("pool", lambda e: e.dma_start(out=wUKV[:], in_=w_ukv_d[l].rearrange("(kc p) n -> p kc n", p=128)), writes=["wUKV"])
                    cosb, sinb = load_rope_tables(sb1)
                    raw = sb1("raw", [128, 4, 512], BF16)
                    sqt = [sb1("sqt%d" % i, [128, 512], BF16) for i in range(2)]
                    rs = sb1("rs", [128, 512], F32)
                    cn = sb1("cn", [128, 4, T], BF16)
                    krr = [sb1("krr%d" % i, [128, 512], BF16) for i in range(2)]
                    rt = [sb1("rt%d" % i, [128, 512], F32) for i in range(4)]
                    proj_rmsnorm(wC, "wC", "kvn", l, cn, "cn", raw, sqt, rs)
                    i = 0
                    for h in range(4):
                        for tc in range(5):
                            t0, n = TCH[tc]
                            bank = 4 + i % 4
                            mm_group(PS[:, bank, 0:n], [(wUKV[:, kc, h * 256:h * 256 + 128], cn[:, kc, t0:t0 + n]) for kc in range(4)],
                                     reads=["wUKV"] + [("cn", kc, tc) for kc in range(4)], writes=[pid(bank)])
                            S.act(lambda e, h=h, bank=bank, t0=t0, n=n: e.activation(out=knope[:, h, t0:t0 + n], in_=PS[:, bank, 0:n], func=AF.Copy),
                                  reads=[pid(bank)], writes=[("knope", h, tc)])
                            i += 1
                    wmv = wUKV[:].rearrange("p kc (h two d) -> p kc h two d", two=2, d=128)
                    for tt in range(18):
                        bank = 4 + i % 4
                        tc = min(tt // 4, 4)
                        mm_group(PS[:, bank, :], [(cn[:, kc, tt * 128:(tt + 1) * 128], wmv[:, kc, :, 1, :]) for kc in range(4)],
                                 reads=["wUKV"] + [("cn", kc, tc) for kc in range(4)], writes=[pid(bank)])
                        S.dve(lambda e, tt=tt, bank=bank: e.tensor_copy(out=vtok[:, tt, :], in_=PS[:, bank, :].rearrange("p (h d) -> p h d", d=128)),
                              reads=[pid(bank)], writes=[("vtok", tt)])
                        i += 1
                    for tc in range(5):
                        t0, n = TCH[tc]
                        bank = 4 + i % 4
                        for hf in range(2):
                            mm_group(PS[hf * 64:(hf + 1) * 64, bank, 0:n], [(wKR[:, k, :], A[:, k, t0:t0 + n]) for k in range(KC)],
                                     reads=["wKR"], writes=[pid(bank)])
                        if tc < 4:
                            kb = tc % 2
                            S.act(lambda e, bank=bank, kb=kb, n=n: e.activation(out=krr[kb][:, 0:n], in_=PS[:, bank, 0:n], func=AF.Copy),
                                  reads=[pid(bank)], writes=[("krr", kb)])
                            rope_chunk(krr[kb][:, 0:n], [("krr", kb)], kr[:, t0:t0 + n], [("kr", tc)], t0, n, cosb, sinb, rt, 2 + kb, kb)
                        else:
                            S.act(lambda e, bank=bank, t0=t0, n=n: e.activation(out=kr[:, t0:t0 + n], in_=PS[:, bank, 0:n], func=AF.Copy),
                                  reads=[pid(bank)], writes=[("kr", tc)])
                        i += 1
                    end_phase()
                cq = sb("cq", [128, 4, T], BF16)
                with contextlib.ExitStack() as st1:
                    def sb1(name, shape, dt):
                        return st1.enter_context(nc.sbuf_tensor(un(name), list(shape), dt))
                    wQ = sb1("wQ", [128, KC, 512], BF16)
                    load_w(wQ, wv, 832, 512, "wQ")
                    raw = sb1("raw", [128, 4, 512], BF16)
                    sqt = [sb1("sqt%d" % i, [128, 512], BF16) for i in range(2)]
                    rs = sb1("rs", [128, 512], F32)
                    proj_rmsnorm(wQ, "wQ", "qn", l, cq, "cq", raw, sqt, rs)
                    end_phase()
                scale = 192.0 ** -0.5
                for hp in range(2):
                    with contextlib.ExitStack() as st1:
                        def sb1(name, shape, dt):
                            return st1.enter_context(nc.sbuf_tensor(un(name), list(shape), dt))
                        wUQ = sb1("wUQ", [128, 4, 768], BF16)
                        S.dma("pool", lambda e: e.dma_start(out=wUQ[:], in_=w_uq_d[l].rearrange("(kc p) n -> p kc n", p=128)), writes=["wUQ"])
                        cosb, sinb = load_rope_tables(sb1)
                        qn = sb1("qn", [128, 2, T], BF16)
                        qr = sb1("qr", [128, T], BF16)
                        qrr = [sb1("qrr%d" % i, [128, 512], BF16) for i in range(2)]
                        rt = [sb1("rt%d" % i, [128, 512], F32) for i in range(4)]
                        pT = [sb1("pT%d" % i, [128, 512], BF16) for i in range(3)]
                        rden = sb1("rden", [128, 512], F32)
                        obr = sb1("obr", [128, 2, T], BF16)
                        i = 0
                        for hf in range(2):
                            h = hp * 2 + hf
                            for tc in range(5):
                                t0, n = TCH[tc]
                                bank = 3 + i % 4
                                mm_group(PS[:, bank, 0:n], [(wUQ[:, kc, h * 192:h * 192 + 128], cq[:, kc, t0:t0 + n]) for kc in range(4)],
                                         reads=["wUQ"], writes=[pid(bank)])
                                S.act(lambda e, hf=hf, bank=bank, t0=t0, n=n: e.activation(out=qn[:, hf, t0:t0 + n], in_=PS[:, bank, 0:n], func=AF.Copy),
                                      reads=[pid(bank)], writes=[("qn", hf, tc)])
                                i += 1
                        for tc in range(5):
                            t0, n = TCH[tc]
                            bank = 3 + i % 4
                            for hf in range(2):
                                h = hp * 2 + hf
                                mm_group(PS[hf * 64:(hf + 1) * 64, bank, 0:n],
                                         [(wUQ[:, kc, h * 192 + 128:h * 192 + 192], cq[:, kc, t0:t0 + n]) for kc in range(4)],
                                         reads=["wUQ"], writes=[pid(bank)])
                            if tc < 4:
                                kb = tc % 2
                                S.act(lambda e, bank=bank, kb=kb, n=n: e.activation(out=qrr[kb][:, 0:n], in_=PS[:, bank, 0:n], func=AF.Copy),
                                      reads=[pid(bank)], writes=[("qrr", kb)])
                                rope_chunk(qrr[kb][:, 0:n], [("qrr", kb)], qr[:, t0:t0 + n], [("qr", tc)], t0, n, cosb, sinb, rt, 7, kb)
                            else:
                                S.act(lambda e, bank=bank, t0=t0, n=n: e.activation(out=qr[:, t0:t0 + n], in_=PS[:, bank, 0:n], func=AF.Copy),
                                      reads=[pid(bank)], writes=[("qr", tc)])
                            i += 1
                        qdeps = [("qn", hf, tc) for hf in range(2) for tc in range(5)] + [("qr", tc) for tc in range(5)]
                        for qc in range(5):
                            q0, n = TCH[qc]
                            tiles = list(range(18)) if qc < 4 else [16, 17]
                            items = [(hf, kt) for hf in range(2) for kt in tiles]

                            def emit_qk(idx, items=items, q0=q0, n=n):
                                hf, kt = items[idx]
                                h = hp * 2 + hf
                                sbank = idx % 3
                                pairs = [(knope[:, h, kt * 128:(kt + 1) * 128], qn[:, hf, q0:q0 + n]),
                                         (kr[hf * 64:(hf + 1) * 64, kt * 128:(kt + 1) * 128], qr[hf * 64:(hf + 1) * 64, q0:q0 + n])]
                                mm_group(PS[:, sbank, 0:n], pairs, reads=qdeps, writes=[pid(sbank)])
                                S.act(lambda e, sbank=sbank: e.activation(out=pT[sbank][:, 0:n], in_=PS[:, sbank, 0:n], func=AF.Exp, scale=scale),
                                      reads=[pid(sbank)], writes=[("pT", sbank)])

                            def emit_pv(idx, items=items, q0=q0, n=n, tiles=tiles):
                                hf, kt = items[idx]
                                h = hp * 2 + hf
                                sbank = idx % 3
                                ob = 3 + hf * 2
                                first = kt == tiles[0]
                                last = kt == tiles[-1]
                                S.pe(lambda e: e.matmul(PS[:, ob, 0:n], vtok[:, kt, h * 128:(h + 1) * 128], pT[sbank][:, 0:n], start=first, stop=last),
                                     reads=[("pT", sbank)], writes=[pid(ob)])
                                S.pe(lambda e: e.matmul(PS[:, ob + 1, 0:n], ones_b[:], pT[sbank][:, 0:n], start=first, stop=last),
                                     reads=[("pT", sbank)], writes=[pid(ob + 1)])
                                if last:
                                    S.dve(lambda e: e.reciprocal(out=rden[:, 0:n], in_=PS[:, ob + 1, 0:n]), reads=[pid(ob + 1)], writes=["rden"])
                                    S.dve(lambda e: e.tensor_tensor(out=obr[:, hf, q0:q0 + n], in0=PS[:, ob, 0:n], in1=rden[:, 0:n], op=ALU.mult),
                                          reads=[pid(ob), "rden"], writes=[("obr", hf)])

                            NI = len(items)
                            for idx in range(min(2, NI)):
                                emit_qk(idx)
                            for idx in range(NI):
                                if idx + 2 < NI:
                                    emit_qk(idx + 2)
                                emit_pv(idx)
                        for hf in range(2):
                            S.dma("sp", lambda e, hf=hf: e.dma_start(out=brT_d[1 * 4 + hp * 2 + hf], in_=obr[:, hf, :]), reads=[("obr", hf)], writes=[("brT", hf)])
                        end_phase()

        def gqa_phase(l):
            with contextlib.ExitStack() as st:
                def sb(name, shape, dt):
                    return st.enter_context(nc.sbuf_tensor(un(name), list(shape), dt))
                wv = w_in_d[l].rearrange("(k p) n -> p k n", p=128)
                wGQ = sb("wGQ", [128, KC, 512], BF16)
                wKV = sb("wKV", [128, KC, 256], BF16)
                load_w(wGQ, wv, 1344, 512, "wGQ")
                load_w(wKV, wv, 576, 256, "wKV")
                cosb, sinb = load_rope_tables(sb)
                gq = sb("gq", [128, 4, T], BF16)
                gk = sb("gk", [128, 2, T], BF16)
                gvt = sb("gvt", [128, 18, 128], BF16)
                rr = [sb("rr%d" % i, [128, 512], BF16) for i in range(2)]
                rt = [sb("rt%d" % i, [128, 512], F32) for i in range(4)]
                i = 0
                for c in range(4):
                    for tc in range(5):
                        t0, n = TCH[tc]
                        bank = 5 + i % 3
                        proj_block(wGQ, "wGQ", c * 128, 128, tc, bank, [])
                        if tc < 4:
                            kb = i % 2
                            S.act(lambda e, bank=bank, kb=kb, n=n: e.activation(out=rr[kb][:, 0:n], in_=PS[:, bank, 0:n], func=AF.Copy),
                                  reads=[pid(bank)], writes=[("rr", kb)])
                            rope_chunk(rr[kb][:, 0:n], [("rr", kb)], gq[:, c, t0:t0 + n], [("gq", c, tc)], t0, n, cosb, sinb, rt, 3 + kb, kb)
                        else:
                            S.act(lambda e, c=c, bank=bank, t0=t0, n=n: e.activation(out=gq[:, c, t0:t0 + n], in_=PS[:, bank, 0:n], func=AF.Copy),
                                  reads=[pid(bank)], writes=[("gq", c, tc)])
                        i += 1
                for kh in range(2):
                    for tc in range(5):
                        t0, n = TCH[tc]
                        bank = 5 + i % 3
                        for hf in range(2):
                            mm_group(PS[hf * 64:(hf + 1) * 64, bank, 0:n], [(wKV[:, k, kh * 64:(kh + 1) * 64], A[:, k, t0:t0 + n]) for k in range(KC)],
                                     reads=[("wKV", 0)], writes=[pid(bank)])
                        if tc < 4:
                            kb = i % 2
                            S.act(lambda e, bank=bank, kb=kb, n=n: e.activation(out=rr[kb][:, 0:n], in_=PS[:, bank, 0:n], func=AF.Copy),
                                  reads=[pid(bank)], writes=[("rr", kb)])
                            rope_chunk(rr[kb][:, 0:n], [("rr", kb)], gk[:, kh, t0:t0 + n], [("gk", kh, tc)], t0, n, cosb, sinb, rt, 3 + kb, kb)
                        else:
                            S.act(lambda e, kh=kh, bank=bank, t0=t0, n=n: e.activation(out=gk[:, kh, t0:t0 + n], in_=PS[:, bank, 0:n], func=AF.Copy),
                                  reads=[pid(bank)], writes=[("gk", kh, tc)])
                        i += 1
                for tg in range(5):
                    nt = 4 if tg < 4 else 2
                    bank = 5 + i % 3
                    def gvmm(e, tg=tg, nt=nt, bank=bank):
                        ins = None
                        for j in range(nt):
                            tt = tg * 4 + j
                            for k in range(KC):
                                ins = e.matmul(PS[:, bank, j * 128:(j + 1) * 128], A[:, k, tt * 128:(tt + 1) * 128], wKV[:, k, 128:256],
                                               start=(k == 0), stop=(k == KC - 1))
                        return ins
                    S.pe(gvmm, reads=[("wKV", 1)], writes=[pid(bank)])
                    S.dve(lambda e, tg=tg, nt=nt, bank=bank: e.tensor_copy(out=gvt[:, tg * 4:tg * 4 + nt, :],
                                                                              in_=PS[:, bank, 0:nt * 128].rearrange("p (j d) -> p j d", d=128)),
                          reads=[pid(bank)], writes=[("gvt", tg)])
                    i += 1
                pT = [sb("pT%d" % i, [128, 512], BF16) for i in range(3)]
                rden = sb("rden", [128, 512], F32)
                obr = sb("obr", [128, 4, T], BF16)
                qdeps = [("gq", c, tc) for c in range(4) for tc in range(5)]
                kdeps = [("gk", kh, tc) for kh in range(2) for tc in range(5)]
                vdeps = [("gvt", tg) for tg in range(5)]
                for qc in range(5):
                    q0, n = TCH[qc]
                    n0 = q0 // 128
                    for hp in range(4):
                        kh = hp // 2
                        ob = 3 + (hp % 2) * 2
                        items = []
                        for hf in range(2):
                            items.append((hf, NLAT, 16, 0, n, None))
                            items.append((hf, NLAT + 128, 17, 0, n, None))
                            if qc < 4:
                                for j in range(n0 - 1, n0 + 5):
                                    if j < 0 or j >= 16:
                                        continue
                                    b_lo = max(j - 1, n0)
                                    b_hi = min(j + 1, n0 + 3)
                                    c0 = (b_lo - n0) * 128
                                    ncol = (b_hi - b_lo + 1) * 128
                                    m0 = (b_lo - (j - 1)) * 128
                                    items.append((hf, j * 128, j, c0, ncol, m0))
                        NI = len(items)
                        firsts = {}
                        lasts = {}
                        for idx, itx in enumerate(items):
                            firsts.setdefault(itx[0], idx)
                            lasts[itx[0]] = idx

                        def emit_qk(idx, items=items, q0=q0, hp=hp, kh=kh):
                            hf, k0, tile, c0, ncol, m0 = items[idx]
                            sbank = idx % 3
                            P = slice(hf * 64, (hf + 1) * 64)
                            pairs = [(gk[P, kh, k0:k0 + 128], gq[P, hp, q0 + c0:q0 + c0 + ncol])]
                            if m0 is not None:
                                pairs.append((ident_b[:], mask_b[:, m0:m0 + ncol]))
                            mm_group(PS[:, sbank, 0:ncol], pairs, reads=kdeps + qdeps, writes=[pid(sbank)])
                            S.act(lambda e, sbank=sbank, ncol=ncol: e.activation(out=pT[sbank][:, 0:ncol], in_=PS[:, sbank, 0:ncol], func=AF.Exp, scale=0.125),
                                  reads=[pid(sbank)], writes=[("pT", sbank)])

                        def emit_pv(idx, items=items, firsts=firsts, lasts=lasts, ob=ob, kh=kh):
                            hf, k0, tile, c0, ncol, m0 = items[idx]
                            sbank = idx % 3
                            P = slice(hf * 64, (hf + 1) * 64)
                            first = firsts[hf] == idx
                            last = lasts[hf] == idx
                            S.pe(lambda e: e.matmul(PS[P, ob, c0:c0 + ncol], gvt[:, tile, kh * 64:(kh + 1) * 64], pT[sbank][:, 0:ncol],
                                                    start=first, stop=last, skip_group_check=True),
                                 reads=vdeps + [("pT", sbank)], writes=[pid(ob)])
                            S.pe(lambda e: e.matmul(PS[P, ob + 1, c0:c0 + ncol], ones_b[:, 0:64], pT[sbank][:, 0:ncol],
                                                    start=first, stop=last, skip_group_check=True),
                                 reads=[("pT", sbank)], writes=[pid(ob + 1)])

                        for idx in range(min(2, NI)):
                            emit_qk(idx)
                        for idx in range(NI):
                            if idx + 2 < NI:
                                emit_qk(idx + 2)
                            emit_pv(idx)
                        S.dve(lambda e, ob=ob, hp=hp, n=n: e.tensor_scalar(out=rden[:, 0:n], in0=PS[:, ob + 1, 0:n], scalar1=esT[:, l, hp:hp + 1], scalar2=None, op0=ALU.add),
                              reads=[pid(ob + 1)], writes=["rden"])
                        S.dve(lambda e, n=n: e.reciprocal(out=rden[:, 0:n], in_=rden[:, 0:n]), reads=["rden"], writes=["rden"])
                        S.dve(lambda e, ob=ob, hp=hp, n=n, q0=q0: e.tensor_tensor(out=obr[:, hp, q0:q0 + n], in0=PS[:, ob, 0:n], in1=rden[:, 0:n], op=ALU.mult),
                              reads=[pid(ob), "rden"], writes=[("obr", hp)])
                for c in range(4):
                    S.dma("sp", lambda e, c=c: e.dma_start(out=brT_d[2 * 4 + c], in_=obr[:, c, :]), reads=[("obr", c)], writes=[("brT", c)])
                end_phase()

        def merge_phase(l):
            with contextlib.ExitStack() as st:
                def sb(name, shape, dt):
                    return st.enter_context(nc.sbuf_tensor(un(name), list(shape), dt))
                B = sb("B", [128, 16, T], BF16)
                brv = brT_d.rearrange("j p t -> p j t")
                for j0 in range(0, 16, 4):
                    S.dma("sp", lambda e, j0=j0: e.dma_start(out=B[:, j0:j0 + 4, :], in_=brv[:, j0:j0 + 4, :]), writes=[("B", j) for j in range(j0, j0 + 4)])
                wv = w_in_d[l].rearrange("(k p) n -> p k n", p=128)
                wgv = wv[:, :, 3392:IN_COLS].rearrange("p k (n f c) -> p k n f c", n=4, f=16)
                wbv = w_br_d[l].rearrange("n (kc p) d -> p n kc d", p=128)
                NW = 4
                wg = [sb("wg%d" % i, [128, KC, 128], BF16) for i in range(NW)]
                wb = [sb("wb%d" % i, [128, 4, 128], BF16) for i in range(NW)]
                sg = [sb("sg%d" % i, [128, 512], F32) for i in range(2)]
                acc = sb("acc", [128, T], F32)
                tmp = [sb("mtmp%d" % i, [128, 512], F32) for i in range(2)]
                yst = [sb("yst%d" % i, [128, T], BF16) for i in range(2)]
                it = 0
                u = 0
                for f in range(16):
                    yi = f % 2
                    for n_ in range(4):
                        wi = u % NW
                        u += 1
                        S.dma("pool", lambda e, wi=wi, f=f, n_=n_: e.dma_start(out=wg[wi][:], in_=wgv[:, :, n_, f, :]), writes=[("wg", wi)])
                        S.dma("pool", lambda e, wi=wi, f=f, n_=n_: e.dma_start(out=wb[wi][:], in_=wbv[:, n_, :, f * 128:(f + 1) * 128]), writes=[("wb", wi)])
                        for tc in range(4 if l == L - 1 else 5):
                            t0, n = TCH[tc]
                            bg = (it % 4)
                            bb = 4 + (it % 4)
                            mm_group(PS[:, bg, 0:n], [(wg[wi][:, k, :], A[:, k, t0:t0 + n]) for k in range(KC)],
                                     reads=[("wg", wi)], writes=[pid(bg)])
                            mm_group(PS[:, bb, 0:n], [(wb[wi][:, kc, :], B[:, n_ * 4 + kc, t0:t0 + n]) for kc in range(4)],
                                     reads=[("wb", wi)] + [("B", n_ * 4 + kc) for kc in range(4)], writes=[pid(bb)])
                            sb_ = it % 2
                            S.act(lambda e, bg=bg, sb_=sb_, n=n: e.activation(out=sg[sb_][:, 0:n], in_=PS[:, bg, 0:n], func=AF.Sigmoid),
                                  reads=[pid(bg)], writes=[("sg", sb_)])
                            if n_ == 0:
                                S.dve(lambda e, bb=bb, sb_=sb_, n=n, t0=t0: e.tensor_tensor(out=acc[:, t0:t0 + n], in0=sg[sb_][:, 0:n], in1=PS[:, bb, 0:n], op=ALU.mult),
                                      reads=[pid(bb), ("sg", sb_)], writes=[("acc", tc)])
                            else:
                                S.dve(lambda e, bb=bb, sb_=sb_, n=n: e.tensor_tensor(out=tmp[sb_][:, 0:n], in0=sg[sb_][:, 0:n], in1=PS[:, bb, 0:n], op=ALU.mult),
                                      reads=[pid(bb), ("sg", sb_)], writes=[("mtmp", sb_)])
                                if n_ < 3:
                                    S.dve(lambda e, n=n, t0=t0, sb_=sb_: e.tensor_tensor(out=acc[:, t0:t0 + n], in0=acc[:, t0:t0 + n], in1=tmp[sb_][:, 0:n], op=ALU.add),
                                          reads=[("acc", tc), ("mtmp", sb_)], writes=[("acc", tc)])
                                else:
                                    S.dve(lambda e, n=n, t0=t0, yi=yi, sb_=sb_: e.tensor_tensor(out=yst[yi][:, t0:t0 + n], in0=acc[:, t0:t0 + n], in1=tmp[sb_][:, 0:n], op=ALU.add),
                                          reads=[("acc", tc), ("mtmp", sb_)], writes=[("yst", yi)])
                            it += 1
                    S.dma("sp", lambda e, f=f, yi=yi: e.dma_start(out=yT_d[f], in_=yst[yi][:]), reads=[("yst", yi)], writes=[("yT", f)])
                end_phase()

        def resid_phase(l, which):
            last = (l == L - 1)
            with contextlib.ExitStack() as st:
                def sb(name, shape, dt):
                    return st.enter_context(nc.sbuf_tensor(un(name), list(shape), dt))
                if which == 0:
                    nk = KC
                    halves = [(0, NLAT, [0, 1, 2, 3])] if last else [(0, T, [0, 1, 2, 3, 4])]
                    src_d = yT_d
                    wview = w_out_d[l].rearrange("(k p) n -> p k n", p=128)
                    mi = 2
                    HL = T

                    def R(j):
                        return A[:, j, :]
                    rgroups = [(A, 0, 4, 0), (A, 4, 8, 0), (A, 8, 12, 0), (A, 12, 16, 0)]
                else:
                    nk = FCH
                    halves = [(0, 1024, [0, 1]), (1024, 1024, [2, 3])] if last else [(0, 1024, [0, 1]), (1024, 1280, [2, 3, 4])]
                    src_d = actT_d
                    wview = w_dn_d[l].rearrange("(f p) n -> p f n", p=128)
                    mi = 5
                    HL = 1280
                    NA = 28
                    Av = A[:].rearrange("p k t -> p (k t)")[:, 0:NA * HL].rearrange("p (j t) -> p j t", t=HL)
                    R2 = sb("R2", [128, nk - NA, HL], BF16)

                    def R(j):
                        return Av[:, j, :] if j < NA else R2[:, j - NA, :]
                    rgroups = [(Av, 0, 7, 0), (Av, 7, 14, 0), (Av, 14, 21, 0), (Av, 21, 28, 0), (R2, 28, 36, NA), (R2, 36, 43, NA)]
                wo = [sb("wo%d" % i, [128, nk, 128], BF16) for i in range(3)]
                xo = [sb("xo%d" % i, [128, HL], F32) for i in range(2)]
                xn = [sb("xn%d" % i, [128, HL], F32) for i in range(2)]
                sqx = [sb("sqx%d" % i, [128, 512], BF16) for i in range(4)]
                rsa = [sb("rsa%d" % i, [128, 512], F32) for i in range(2)]
                rsb = [sb("rsb%d" % i, [128, 512], F32) for i in range(2)]
                it = 0
                mb = 0
                sc = 0
                pend = []

                def flush(keep):
                    while len(pend) > keep:
                        si, tc, n, o = pend.pop(0)
                        S.pe(lambda e, si=si, tc=tc, n=n, o=o: e.matmul(PS[:, 3 + tc, 0:n], ones_b[:], sqx[si][:, 0:n], start=(o == 0), stop=(o == 15)),
                             reads=[("sqx", si)], writes=[pid(3 + tc)])

                for hi_, (h0, hl, tcs) in enumerate(halves):
                    srcv = src_d.rearrange("j p t -> p j t")
                    for (dstv, j0, j1, joff) in rgroups:
                        S.dma("sp", lambda e, dstv=dstv, j0=j0, j1=j1, joff=joff, h0=h0, hl=hl: e.dma_start(
                            out=dstv[:, j0 - joff:j1 - joff, 0:hl], in_=srcv[:, j0:j1, h0:h0 + hl]),
                            writes=[("R", j) for j in range(j0, j1)])
                    for o in range(16):
                        wi = it % 3
                        xb = it % 2
                        S.dma("pool", lambda e, wi=wi, o=o: e.dma_start(out=wo[wi][:], in_=wview[:, :, o * 128:(o + 1) * 128]), writes=[("wo", wi)])
                        S.dma("sp", lambda e, xb=xb, o=o, h0=h0, hl=hl: e.dma_start(out=xo[xb][:, 0:hl], in_=xT_d[o][:, h0:h0 + hl]),
                              writes=[("xo", xb)])
                        for tci, tc in enumerate(tcs):
                            t0, n = TCH[tc]
                            bank = mb % 3
                            mb += 1
                            mm_group(PS[:, bank, 0:n], [(wo[wi][:, j, :], R(j)[:, t0 - h0:t0 - h0 + n]) for j in range(nk)],
                                     reads=[("wo", wi)] + [("R", j) for j in range(nk)], writes=[pid(bank)])
                            s = 1 if tc == 4 else 0
                            gt = modT[:, l, mi, o, s:s + 1]
                            S.dve(lambda e, bank=bank, n=n, xb=xb, t0=t0, h0=h0, gt=gt: e.scalar_tensor_tensor(
                                out=xn[xb][:, t0 - h0:t0 - h0 + n], in0=PS[:, bank, 0:n], scalar=gt, in1=xo[xb][:, t0 - h0:t0 - h0 + n],
                                op0=ALU.mult, op1=ALU.add), reads=[pid(bank), ("xo", xb)], writes=[("xn", xb, tc)])
                            si = sc % 4
                            sc += 1
                            S.act(lambda e, si=si, n=n, xb=xb, t0=t0, h0=h0: e.activation(out=sqx[si][:, 0:n], in_=xn[xb][:, t0 - h0:t0 - h0 + n], func=AF.Square),
                                  reads=[("xn", xb, tc)], writes=[("sqx", si)])
                            pend.append((si, tc, n, o))
                            flush(2)
                        S.dma("sp", lambda e, xb=xb, o=o, h0=h0, hl=hl: e.dma_start(out=xT_d[o][:, h0:h0 + hl], in_=xn[xb][:, 0:hl]),
                              reads=[("xn", xb, tc) for tc in tcs], writes=[("xTo", o)])
                        it += 1
                    flush(0)
                    for tci, tc in enumerate(tcs):
                        t0, n = TCH[tc]
                        rb = tci % 2
                        S.act(lambda e, tc=tc, n=n, rb=rb: e.activation(out=rsa[rb][:, 0:n], in_=PS[:, 3 + tc, 0:n], func=AF.Sqrt, scale=1.0 / D, bias=eps_t[:, 0:1]),
                              reads=[pid(3 + tc)], writes=[("rsa", rb)])
                        S.dve(lambda e, n=n, rb=rb: e.reciprocal(out=rsb[rb][:, 0:n], in_=rsa[rb][:, 0:n]), reads=[("rsa", rb)], writes=[("rsb", rb)])
                        S.dma("sp", lambda e, t0=t0, n=n, rb=rb: e.dma_start(out=rs_d[:, t0:t0 + n], in_=rsb[rb][:, 0:n]), reads=[("rsb", rb)], writes=[("rsd", tc)])
                end_phase()

        def ffn_up_phase(l):
            with contextlib.ExitStack() as st:
                def sb(name, shape, dt):
                    return st.enter_context(nc.sbuf_tensor(un(name), list(shape), dt))
                wv = w_up_d[l].rearrange("(k p) n -> p k n", p=128).rearrange("p k (two j c) -> p k two j c", two=2, j=FCH)
                wu = [sb("wu%d" % i, [128, KC, 2, 128], BF16) for i in range(3)]
                ua = [sb("ua%d" % i, [128, FFW], F32) for i in range(2)]
                ug = [sb("ug%d" % i, [128, FFW], F32) for i in range(2)]
                va = [sb("va%d" % i, [128, FFW], F32) for i in range(2)]
                vg = [sb("vg%d" % i, [128, FFW], F32) for i in range(2)]
                ao = [sb("ao%d" % i, [128, T], BF16) for i in range(2)]
                do_ada = l < L - 1
                if do_ada:
                    wa2 = [sb("wa2_%d" % i, [128, KC, 128], BF16) for i in range(3)]
                    wav = w_ada_d[l + 1].rearrange("(k p) n -> p k n", p=128)
                ada_j = [0]

                def ada_block():
                    j = ada_j[0]
                    ada_j[0] += 1
                    wi2 = j % 3
                    S.dma("pool", lambda e, wi2=wi2, j=j: e.dma_start(out=wa2[wi2][:], in_=wav[:, :, j * 128:(j + 1) * 128]), writes=[("wa2", wi2)])
                    def adamm(e, wi2=wi2, j=j):
                        ins = None
                        for k in range(KC):
                            ins = e.matmul(PS[:, 7, j * 2:j * 2 + 2], wa2[wi2][:, k, :], silb[:, :, k], start=(k == 0), stop=(k == KC - 1))
                        return ins
                    S.pe(adamm, reads=[("wa2", wi2)], writes=[pid(7)])
                for i in range(2):
                    S.dve(lambda e, i=i: e.memset(ua[i][:], 0.0), writes=[("ua", i)])
                    S.dve(lambda e, i=i: e.memset(ug[i][:], 0.0), writes=[("ug", i)])

                def fcol0(tc):
                    t0, n = TCH[tc]
                    return (1 + t0) if tc < 4 else (1 + NLAT + 2)

                it = 0
                for f in range(FCH):
                    wi = f % 3
                    ub_ = f % 2
                    for two in range(2):
                        S.dma("pool", lambda e, wi=wi, f=f, two=two: e.dma_start(out=wu[wi][:, :, two, :], in_=wv[:, :, two, f, :]), writes=[("wu", wi)])
                    for tc in range(4 if l == L - 1 else 5):
                        t0, n = TCH[tc]
                        f0 = fcol0(tc)
                        for two, dst, nm in ((0, ua, "ua"), (1, ug, "ug")):
                            bank = it % 7
                            mm_group(PS[:, bank, 0:n], [(wu[wi][:, k, two, :], A[:, k, t0:t0 + n]) for k in range(KC)],
                                     reads=[("wu", wi), "Aall"], writes=[pid(bank)])
                            if two == 0:
                                S.act(lambda e, dst=dst, bank=bank, n=n, f0=f0, ub_=ub_: e.activation(out=dst[ub_][:, f0:f0 + n], in_=PS[:, bank, 0:n], func=AF.Copy),
                                      reads=[pid(bank)], writes=[(nm, ub_)])
                            else:
                                S.dve(lambda e, dst=dst, bank=bank, n=n, f0=f0, ub_=ub_: e.tensor_copy(out=dst[ub_][:, f0:f0 + n], in_=PS[:, bank, 0:n]),
                                      reads=[pid(bank)], writes=[(nm, ub_)])
                            it += 1
                    if do_ada:
                        for _ in range(3 if f < 10 else 2):
                            if ada_j[0] < 96:
                                ada_block()
                    W = FFW - 2
                    for two, src, dst, nm, vn in ((0, ua, va, "ua", "va"), (1, ug, vg, "ug", "vg")):
                        ch = two * FCH + f
                        w0, w1, w2 = (vcol("fcw", l * 258 + tap * 86 + ch) for tap in range(3))
                        S.dve(lambda e, src=src, dst=dst, w0=w0, ub_=ub_: e.tensor_scalar(out=dst[ub_][:, 1:1 + W], in0=src[ub_][:, 0:W], scalar1=w0, scalar2=None, op0=ALU.mult),
                              reads=[(nm, ub_), "vT"], writes=[(vn, ub_)])
                        S.dve(lambda e, src=src, dst=dst, w1=w1, ub_=ub_: e.scalar_tensor_tensor(out=dst[ub_][:, 1:1 + W], in0=src[ub_][:, 1:1 + W], scalar=w1, in1=dst[ub_][:, 1:1 + W],
                                                                                    op0=ALU.mult, op1=ALU.add), reads=[(nm, ub_), (vn, ub_), "vT"], writes=[(vn, ub_)])
                        S.dve(lambda e, src=src, dst=dst, w2=w2, ub_=ub_: e.scalar_tensor_tensor(out=dst[ub_][:, 1:1 + W], in0=src[ub_][:, 2:2 + W], scalar=w2, in1=dst[ub_][:, 1:1 + W],
                                                                                    op0=ALU.mult, op1=ALU.add), reads=[(nm, ub_), (vn, ub_), "vT"], writes=[(vn, ub_)])
                    S.act(lambda e, ub_=ub_: e.activation(out=vg[ub_][:, 1:1 + W], in_=vg[ub_][:, 1:1 + W], func=AF.Silu), reads=[("vg", ub_)], writes=[("vg", ub_)])
                    S.dve(lambda e, ub_=ub_: e.tensor_tensor(out=ao[ub_][:, 0:NLAT], in0=va[ub_][:, 1:1 + NLAT], in1=vg[ub_][:, 1:1 + NLAT], op=ALU.mult),
                          reads=[("va", ub_), ("vg", ub_)], writes=[("ao", ub_)])
                    c0 = 1 + NLAT + 2
                    S.dve(lambda e, ub_=ub_, c0=c0: e.tensor_tensor(out=ao[ub_][:, NLAT:T], in0=va[ub_][:, c0:c0 + NCTX], in1=vg[ub_][:, c0:c0 + NCTX], op=ALU.mult),
                          reads=[("va", ub_), ("vg", ub_), ("ao", ub_)], writes=[("ao", ub_)])
                    S.dma("sp", lambda e, f=f, ub_=ub_: e.dma_start(out=actT_d[f], in_=ao[ub_][:]), reads=[("ao", ub_)], writes=[("srcd", f)])
                if do_ada:
                    assert ada_j[0] == 96
                    ada_finish(l + 1, 7)
                end_phase()

        eps_t = gsb("eps_t", [128, 1], F32)
        S.dve(lambda e: e.memset(eps_t[:], EPS), writes=["eps"])
        prologue()
        phases = []
        for l in range(L):
            phases += [("n1", l), ("conv", l), ("pool", l), ("mla", l), ("gqa", l), ("merge", l), ("wout", l), ("n2", l), ("up", l), ("down", l)]
        for (ph, l) in phases:
            if done["stop"]:
                break
            if ph == "n1":
                if l == 0:
                    norm_phase(l, 0, have_stats=False)
            elif ph == "conv":
                conv_phase(l, fused_norm=(l > 0))
            elif ph == "pool":
                pool_phase(l)
            elif ph == "mla":
                mla_phase(l)
            elif ph == "gqa":
                gqa_phase(l)
            elif ph == "merge":
                merge_phase(l)
            elif ph == "wout":
                resid_phase(l, 0)
            elif ph == "n2":
                norm_phase(l, 1, have_stats=True)
            elif ph == "up":
                ffn_up_phase(l)
            elif ph == "down":
                resid_phase(l, 1)
            check_stop((ph, l))
        norm_phase(L - 1, 2, have_stats=True)
    return nc


_CACHE = {}


def kernel(**inputs):
    inp = {k: np.asarray(v) for k, v in inputs.items()}
    consts = _make_consts()
    if "nc" not in _CACHE:
        _CACHE["nc"] = build_program()
    nc = _CACHE["nc"]
    shared = {
        "sink": np.ascontiguousarray(inp["gqa_sink"], dtype=np.float32).reshape(-1),
        "w_ada": inp["w_ada"], "w_in": inp["w_in"], "mla_w_uq": inp["mla_w_uq"], "mla_w_ukv": inp["mla_w_ukv"],
        "pool_w": inp["pool_w"], "w_branch": inp["w_branch"], "w_out": inp["w_out"],
        "ffn_w_up": inp["ffn_w_up"], "ffn_w_down": inp["ffn_w_down"],
    }
    shared.update(consts)
    shared = {k: np.ascontiguousarray(v, dtype=np.float32) for k, v in shared.items()}
    in_maps = []
    for b in range(8):
        m = dict(shared)
        m["x"] = np.ascontiguousarray(inp["x"][b], dtype=np.float32)
        m["ctx"] = np.ascontiguousarray(inp["ctx"][b], dtype=np.float32)
        m["vecs"] = _make_vecs(inp, b)
        in_maps.append(m)
    res = run_bass_kernel_spmd(nc, in_maps, core_ids=list(range(8)))
    if DEBUG["on"]:
        _CACHE["last"] = res
    out = np.stack([np.asarray(r["out"], dtype=np.float32) for r in res.results], 0)
    return out
```
